# Optimizing a Trainium2 kernel written in Bass

```python
import jax, jax.numpy as jnp
from jax import lax
import numpy as np

D_MODEL = 2048
BATCH = 1
SEQ = 8192
DEPTH = 2
DEC_BATCH = 128
DEC_SEQ = 4
PAST_LEN = 2048
PAGE_SIZE = 128

D_A = D_MODEL // 2
HD_A = 64
H_A = D_A // HD_A
LORA_W = 64
LORA_A = 64
LORA_G = 128
D_B = D_MODEL // 4
HD_B = 64
H_B = D_B // HD_B
D_C = D_MODEL - D_A - D_B
POOL_WINDOWS = (2, 4, 8, 16)
N_POOL_GROUPS = len(POOL_WINDOWS)
C_POOL = D_C // N_POOL_GROUPS
POOL_BUF = max(POOL_WINDOWS) - 1
RWKV_COLS = 3 * D_A + LORA_W + LORA_A + LORA_G
IN_COLS = RWKV_COLS + 3 * D_B + D_C
D_FF = -(-8 * D_MODEL // (3 * 256)) * 256
D_PLE = 256
SB_BLOCK = 128
EPS = 1e-6
GN_EPS = 64e-5

kernel_name = 'hybrid_rwkv7_stickbreak_pool_decode_step'


def rmsnorm(x, g):
    xf = x.astype(jnp.float32)
    y = xf * lax.rsqrt(jnp.mean(xf * xf, axis=-1, keepdims=True) + EPS)
    return (y * g.astype(jnp.float32)).astype(x.dtype)


def rwkv7_mixer(u, prev, wkv0, mu, w0, w_lora_up, a0, a_lora_up, g_lora_up,
                k_k, k_a, r_k, ln_w, ln_b):
    B, T, _ = u.shape
    f32 = jnp.float32
    uf = u.astype(f32)
    shifted = jnp.concatenate([prev.astype(f32)[:, None], uf[:, :-1]], axis=1)
    xs = uf + (shifted - uf) * mu
    i1, i2, i3 = D_A, 2 * D_A, 3 * D_A
    i4 = i3 + LORA_W
    i5 = i4 + LORA_A
    r, k, v, xw, xa, xg = jnp.split(xs, [i1, i2, i3, i4, i5], axis=-1)
    w = -jax.nn.softplus(-(w0 + jnp.tanh(xw) @ w_lora_up)) - 0.5
    decay = jnp.exp(-jnp.exp(w))
    a = jax.nn.sigmoid(a0 + xa @ a_lora_up)
    g = jax.nn.sigmoid(xg) @ g_lora_up
    heads = lambda z: z.reshape(B, T, H_A, HD_A)
    kk = heads(k * k_k)
    kk = kk / jnp.maximum(jnp.sqrt(jnp.sum(kk * kk, axis=-1, keepdims=True)), 1e-12)
    k = k * (1.0 + (a - 1.0) * k_a)
    r, k, v, decay, a = heads(r), heads(k), heads(v), heads(decay), heads(a)

    def step(S, inp):
        r_t, d_t, k_t, v_t, kk_t, a_t = inp
        sa = jnp.einsum('bhvk,bhk->bhv', S, kk_t)
        S = (S * d_t[:, :, None, :]
             - sa[..., None] * (kk_t * a_t)[:, :, None, :]
             + v_t[..., None] * k_t[:, :, None, :])
        return S, jnp.einsum('bhvk,bhk->bhv', S, r_t)

    seq = tuple(jnp.moveaxis(z, 1, 0) for z in (r, decay, k, v, kk, a))
    S_fin, ys = lax.scan(step, wkv0.astype(f32), seq)
    y = jnp.moveaxis(ys, 0, 1)
    mean = jnp.mean(y, axis=-1, keepdims=True)
    var = jnp.mean(jnp.square(y - mean), axis=-1, keepdims=True)
    yn = ((y - mean) * lax.rsqrt(var + GN_EPS)).reshape(B, T, D_A) * ln_w + ln_b
    bonus = jnp.sum(r * k * r_k, axis=-1, keepdims=True) * v
    out = (yn + bonus.reshape(B, T, D_A)) * g
    return out, S_fin


def stick_breaking(q, k, v, q_pos, k_pos, sb_bias):
    B, Tq, H, D = q.shape
    qb = SB_BLOCK if Tq % SB_BLOCK == 0 else Tq
    nb = Tq // qb
    qs = jnp.moveaxis(q.reshape(B, nb, qb, H, D), 1, 0)
    ps = q_pos.reshape(nb, qb)
    kf = k.astype(jnp.float32)
    vf = v.astype(jnp.float32)
    bias = sb_bias.astype(jnp.float32)[None, :, None, None]
    scale = D ** -0.5

    def block(args):
        qblk, pblk = args
        z = jnp.einsum('bqhd,bkhd->bhqk', qblk.astype(jnp.float32), kf) * scale + bias
        mask = k_pos[None, :] < pblk[:, None]
        log_keep = jnp.where(mask, jax.nn.log_sigmoid(-z), 0.0)
        later = lax.cumsum(log_keep, axis=3, reverse=True) - log_keep
        att = jnp.where(mask, jnp.exp(jax.nn.log_sigmoid(z) + later), 0.0)
        return jnp.einsum('bhqk,bkhd->bqhd', att, vf)

    out = lax.map(block, (qs, ps))
    return jnp.moveaxis(out, 0, 1).reshape(B, Tq, H * D)


def multiscale_pool(u, prefix, start_pos, w_pool, pool_scale):
    B, T, _ = u.shape
    f32 = jnp.float32
    uf = u.astype(f32)
    ext = jnp.concatenate([prefix.astype(f32), uf], axis=1)
    c = jnp.concatenate([jnp.zeros((B, 1, D_C), f32), jnp.cumsum(ext, axis=1)], axis=1)
    pos = start_pos + jnp.arange(T)
    end = c[:, POOL_BUF + 1:]
    pooled = []
    for gi, w in enumerate(POOL_WINDOWS):
        sl = slice(gi * C_POOL, (gi + 1) * C_POOL)
        start = c[:, POOL_BUF + 1 - w: POOL_BUF + 1 - w + T, sl]
        cnt = jnp.minimum(w, pos + 1).astype(f32)[None, :, None]
        pooled.append((end[..., sl] - start) / cnt - uf[..., sl])
    pooled = jnp.stack(pooled, axis=2)
    out = jnp.einsum('btgc,gcd->btgd', pooled, w_pool).reshape(B, T, D_C) * pool_scale
    return out, ext[:, -POOL_BUF:]


def decoder_layer(h, p_i, prev_shift, wkv0, pool_prefix, k_past, v_past, start_pos,
                  g_mix, w_in, mu_shift, w0, w_lora_up, a0, a_lora_up, g_lora_up,
                  k_k, k_a, r_k, ln_w, ln_b, sb_bias, w_pool, pool_scale, w_o,
                  g_ffn, w_gate, w_up, w_down, g_ple, w_pg, w_pp):
    B, T, _ = h.shape
    u = rmsnorm(h, g_mix) @ w_in
    u_a = u[..., :RWKV_COLS]
    u_b = u[..., RWKV_COLS:RWKV_COLS + 3 * D_B]
    u_c = u[..., RWKV_COLS + 3 * D_B:]
    y_a, wkv_new = rwkv7_mixer(u_a, prev_shift, wkv0, mu_shift, w0, w_lora_up, a0,
                               a_lora_up, g_lora_up, k_k, k_a, r_k, ln_w, ln_b)
    q, k, v = (z.reshape(B, T, H_B, HD_B) for z in jnp.split(u_b, 3, axis=-1))
    q_pos = start_pos + jnp.arange(T)
    if k_past is None:
        k_all, v_all, k_pos = k, v, q_pos
    else:
        k_all = jnp.concatenate([k_past.astype(k.dtype), k], axis=1)
        v_all = jnp.concatenate([v_past.astype(v.dtype), v], axis=1)
        k_pos = jnp.arange(k_past.shape[1] + T)
    y_b = stick_breaking(q, k_all, v_all, q_pos, k_pos, sb_bias)
    y_c, pool_new = multiscale_pool(u_c, pool_prefix, start_pos, w_pool, pool_scale)
    mix = jnp.concatenate([y_a, y_b, y_c], axis=-1).astype(h.dtype)
    h = h + mix @ w_o
    xn = rmsnorm(h, g_ffn)
    h = h + (jax.nn.silu(xn @ w_gate) * (xn @ w_up)) @ w_down
    xn = rmsnorm(h, g_ple)
    h = h + jax.nn.sigmoid(xn @ w_pg) * (p_i.astype(h.dtype) @ w_pp)
    return (h, k, v, wkv_new.astype(h.dtype), u_a[:, -1], pool_new.astype(h.dtype))


def setup_inputs(seed: int = 0) -> dict:
    key = jax.random.key(seed)
    ks = iter(jax.random.split(key, 48))
    f32 = jnp.float32
    nrm = lambda shape, s=1.0: jax.random.normal(next(ks), shape, f32) * s
    uni = lambda shape, lo, hi: jax.random.uniform(next(ks), shape, f32, lo, hi)
    n_pages = PAST_LEN // PAGE_SIZE
    n_pool = (DEC_BATCH * n_pages * 5 + 3) // 4
    page_table = jax.random.permutation(next(ks), n_pool)[:DEC_BATCH * n_pages]
    page_table = page_table.reshape(DEC_BATCH, n_pages).astype(jnp.int32)
    L = DEPTH
    return {
        'x_prompt': nrm((BATCH, SEQ, D_MODEL)),
        'x_sample': nrm((DEC_BATCH, DEC_SEQ, D_MODEL)),
        'p_prompt': nrm((DEPTH, BATCH, SEQ, D_PLE)),
        'p_sample': nrm((DEPTH, DEC_BATCH, DEC_SEQ, D_PLE)),
        'cache_k': nrm((DEPTH, n_pool, PAGE_SIZE, H_B, HD_B)),
        'cache_v': nrm((DEPTH, n_pool, PAGE_SIZE, H_B, HD_B)),
        'page_table': page_table,
        'state_wkv': nrm((DEPTH, DEC_BATCH, H_A, HD_A, HD_A), 0.3),
        'state_shift': nrm((DEPTH, DEC_BATCH, RWKV_COLS)),
        'state_pool': nrm((DEPTH, DEC_BATCH, POOL_BUF, D_C)),
        'g_mix': 1.0 + nrm((L, D_MODEL), 0.02),
        'w_in': nrm((L, D_MODEL, IN_COLS), D_MODEL ** -0.5),
        'mu_shift': uni((L, RWKV_COLS), 0.0, 1.0),
        'w0': uni((L, D_A), -5.0, 0.0),
        'w_lora_up': nrm((L, LORA_W, D_A), 0.5 * LORA_W ** -0.5),
        'a0': nrm((L, D_A), 0.5),
        'a_lora_up': nrm((L, LORA_A, D_A), 0.5 * LORA_A ** -0.5),
        'g_lora_up': nrm((L, LORA_G, D_A), LORA_G ** -0.5),
        'k_k': 1.0 + nrm((L, D_A), 0.1),
        'k_a': 1.0 + nrm((L, D_A), 0.1),
        'r_k': nrm((L, H_A, HD_A), 0.1),
        'ln_w': 1.0 + nrm((L, D_A), 0.02),
        'ln_b': nrm((L, D_A), 0.02),
        'sb_bias': uni((L, H_B), -8.5, -7.0),
        'w_pool': nrm((L, N_POOL_GROUPS, C_POOL, C_POOL), C_POOL ** -0.5),
        'pool_scale': 1.0 + nrm((L, D_C), 0.1),
        'w_o': nrm((L, D_MODEL, D_MODEL), D_MODEL ** -0.5),
        'g_ffn': 1.0 + nrm((L, D_MODEL), 0.02),
        'w_gate': nrm((L, D_MODEL, D_FF), D_MODEL ** -0.5),
        'w_up': nrm((L, D_MODEL, D_FF), D_MODEL ** -0.5),
        'w_down': nrm((L, D_FF, D_MODEL), D_FF ** -0.5),
        'g_ple': 1.0 + nrm((L, D_MODEL), 0.02),
        'w_pg': nrm((L, D_MODEL, D_MODEL), D_MODEL ** -0.5),
        'w_pp': nrm((L, D_PLE, D_MODEL), D_PLE ** -0.5),
        'g_final': 1.0 + nrm((D_MODEL,), 0.02),
    }


def reference(x_prompt, x_sample, p_prompt, p_sample, cache_k, cache_v, page_table,
              state_wkv, state_shift, state_pool, g_mix, w_in, mu_shift, w0, w_lora_up,
              a0, a_lora_up, g_lora_up, k_k, k_a, r_k, ln_w, ln_b, sb_bias, w_pool,
              pool_scale, w_o, g_ffn, w_gate, w_up, w_down, g_ple, w_pg, w_pp, g_final):
    n_pages = PAST_LEN // PAGE_SIZE
    bp = x_prompt.shape[0]
    bs = x_sample.shape[0]
    dt = x_prompt.dtype
    zero_shift = jnp.zeros((bp, RWKV_COLS), dt)
    zero_wkv = jnp.zeros((bp, H_A, HD_A, HD_A), dt)
    zero_pool = jnp.zeros((bp, POOL_BUF, D_C), dt)
    hp, hs = x_prompt, x_sample
    kp, vp, ksm, vsm, wp, ws, sp, ss, pp, ps = ([] for _ in range(10))
    for i in range(DEPTH):
        lw = (g_mix[i], w_in[i], mu_shift[i], w0[i], w_lora_up[i], a0[i], a_lora_up[i],
              g_lora_up[i], k_k[i], k_a[i], r_k[i], ln_w[i], ln_b[i], sb_bias[i], w_pool[i],
              pool_scale[i], w_o[i], g_ffn[i], w_gate[i], w_up[i], w_down[i], g_ple[i],
              w_pg[i], w_pp[i])
        hp, k_n, v_n, wkv_n, sh_n, pl_n = decoder_layer(
            hp, p_prompt[i], zero_shift, zero_wkv, zero_pool, None, None, 0, *lw)
        kp.append(k_n); vp.append(v_n); wp.append(wkv_n); sp.append(sh_n); pp.append(pl_n)
        k_past = cache_k[i][page_table].reshape(bs, n_pages * PAGE_SIZE, H_B, HD_B)
        v_past = cache_v[i][page_table].reshape(bs, n_pages * PAGE_SIZE, H_B, HD_B)
        hs, k_n, v_n, wkv_n, sh_n, pl_n = decoder_layer(
            hs, p_sample[i], state_shift[i], state_wkv[i], state_pool[i],
            k_past, v_past, PAST_LEN, *lw)
        ksm.append(k_n); vsm.append(v_n); ws.append(wkv_n); ss.append(sh_n); ps.append(pl_n)
    y_prompt = rmsnorm(hp, g_final)
    y_sample = rmsnorm(hs, g_final)
    return (y_prompt, y_sample,
            jnp.stack(kp), jnp.stack(vp), jnp.stack(ksm), jnp.stack(vsm),
            jnp.stack(wp), jnp.stack(ws), jnp.stack(sp), jnp.stack(ss),
            jnp.stack(pp), jnp.stack(ps))
```

```python
import numpy as np
import ml_dtypes
import concourse.bass as bass
import concourse.mybir as mybir
from concourse.bass_utils import run_bass_kernel_spmd

F32 = mybir.dt.float32
BF16 = mybir.dt.bfloat16
I32 = mybir.dt.int32
AF = mybir.ActivationFunctionType
ALU = mybir.AluOpType
NCORES = 8
RWDBG = 3
D = 2048
EPS = 1e-6


class Buf:
    _n = 0

    def __init__(self, name):
        Buf._n += 1
        self.name = f"{name}#{Buf._n}"
        self.sem = None
        self.sem_total = 0
        self.last_write = None
        self.readers = []


class Sched:
    ENG = ("pe", "act", "dve", "pool", "sp")

    def __init__(self, nc, sems):
        self.nc = nc
        self.free_sems = list(sems)
        self.q = {e: [] for e in self.ENG}
        self.esem = {e: self.free_sems.pop() for e in self.ENG}
        self.ecnt = {e: 0 for e in self.ENG}
        self.waited = {e: {} for e in self.ENG}
        self.all_dma_bufs = []

    def _need(self, eng, tok, deps):
        if tok is None:
            return
        sem, val = tok
        k = id(sem)
        if deps.get(k, (None, 0))[1] < val:
            deps[k] = (sem, val)

    def _emit_waits(self, eng, deps):
        for k, (sem, val) in deps.items():
            if self.waited[eng].get(k, 0) >= val:
                continue
            self.waited[eng][k] = val
            self.q[eng].append(lambda e, sem=sem, val=val: e.wait_ge(sem, val))

    def _deps(self, eng, reads, writes):
        deps = {}
        for b in reads:
            self._need(eng, b.last_write, deps)
        for b in writes:
            self._need(eng, b.last_write, deps)
            for t in b.readers:
                self._need(eng, t, deps)
        self._emit_waits(eng, deps)

    def op(self, eng, fn, reads=(), writes=()):
        self._deps(eng, reads, writes)
        self.ecnt[eng] += 1
        sem = self.esem[eng]
        tok = (sem, self.ecnt[eng])
        self.q[eng].append(lambda e, fn=fn, sem=sem: fn(e).then_inc(sem, 1))
        for b in reads:
            b.readers.append(tok)
        for b in writes:
            b.last_write = tok
            b.readers = []
        return tok

    def dma(self, fn, reads=(), writes=(), eng="sp", inc=16):
        self._deps(eng, reads, writes)
        d = writes[0]
        if d.sem is None:
            d.sem = self.free_sems.pop()
            self.all_dma_bufs.append(d)
        d.sem_total += inc
        tok = (d.sem, d.sem_total)
        self.q[eng].append(lambda e, fn=fn, sem=d.sem, inc=inc: fn(e).then_inc(sem, inc))
        for b in reads:
            b.readers.append(tok)
        for b in writes:
            b.last_write = tok
            b.readers = []
        return tok

    def barrier(self):
        for en in self.ENG:
            deps = {}
            for e2 in self.ENG:
                if self.ecnt[e2]:
                    self._need(en, (self.esem[e2], self.ecnt[e2]), deps)
            for b in self.all_dma_bufs:
                self._need(en, (b.sem, b.sem_total), deps)
            self._emit_waits(en, deps)

    def finish(self):
        for b in self.all_dma_bufs:
            self.q["sp"].append(lambda e, s=b.sem, v=b.sem_total: e.wait_ge(s, v))
        for en in self.ENG:
            if en != "sp":
                s, v = self.esem[en], self.ecnt[en]
                if v:
                    self.q["sp"].append(lambda e, s=s, v=v: e.wait_ge(s, v))
        nc = self.nc
        with nc.Block() as block:
            @block.tensor
            def _(e):
                for f in self.q["pe"]:
                    f(e)

            @block.scalar
            def _(e):
                for f in self.q["act"]:
                    f(e)

            @block.vector
            def _(e):
                for f in self.q["dve"]:
                    f(e)

            @block.gpsimd
            def _(e):
                for f in self.q["pool"]:
                    f(e)

            @block.sync
            def _(e):
                for f in self.q["sp"]:
                    f(e)


class Ctx:
    def __init__(self, nc, stack):
        self.nc = nc
        self.stack = stack

    def sb(self, name, shape, dt):
        return self.stack.enter_context(self.nc.sbuf_tensor(name, list(shape), dt))

    def ps(self, name, shape, dt=F32):
        return self.stack.enter_context(self.nc.psum_tensor(name, list(shape), dt))


NIN = 1024
NFF = 704
XW = 8192


def build_program(T, SEQ, NS, NPG, NPOOL):
    import contextlib
    nc = bass.Bass("TRN2", target_bir_lowering=False)
    KT = D // 128
    KF = 5632 // 128
    NSEQ = NS // 4

    xT = nc.dram_tensor("xT", [256, T], F32, kind="ExternalInput")
    pT = nc.dram_tensor("pT", [2, 256, T], F32, kind="ExternalInput")
    w_in = nc.dram_tensor("w_in", [2, D, NIN], F32, kind="ExternalInput")
    w_o = nc.dram_tensor("w_o", [2, D, 256], F32, kind="ExternalInput")
    w_gu = nc.dram_tensor("w_gu", [2, D, 2 * NFF], F32, kind="ExternalInput")
    w_dn = nc.dram_tensor("w_dn", [2, 5632, 256], F32, kind="ExternalInput")
    w_pg = nc.dram_tensor("w_pg", [2, D, 256], F32, kind="ExternalInput")
    w_pp = nc.dram_tensor("w_pp", [2, 256, 256], F32, kind="ExternalInput")
    gvec = nc.dram_tensor("gvec", [6, 128, KT], F32, kind="ExternalInput")
    gfin = nc.dram_tensor("gfin", [128, 2], F32, kind="ExternalInput")
    spool = nc.dram_tensor("spool", [2, NSEQ, 15, 128], F32, kind="ExternalInput")
    spT = nc.dram_tensor("spT", [2, 128, NSEQ, 15], F32, kind="ExternalInput")
    selw_in = nc.dram_tensor("selw", [128, 4], F32, kind="ExternalInput")
    wpool_in = nc.dram_tensor("wpool", [2, 128, 64], F32, kind="ExternalInput")
    psc_in = nc.dram_tensor("psc", [64, 2], F32, kind="ExternalInput")
    sbb_in = nc.dram_tensor("sbb", [128, 2], F32, kind="ExternalInput")
    ck = [nc.dram_tensor(f"ck{l}", [NPOOL * 128, 64], F32, kind="ExternalInput") for l in range(2)]
    cv = [nc.dram_tensor(f"cv{l}", [NPOOL * 128, 64], F32, kind="ExternalInput") for l in range(2)]
    ptab = nc.dram_tensor("ptab", [1, NSEQ * NPG], I32, kind="ExternalInput")
    rw_mu = nc.dram_tensor("rw_mu", [128, 10], F32, kind="ExternalInput")
    rw_prm = nc.dram_tensor("rw_prm", [128, 14], F32, kind="ExternalInput")
    rw_lw = nc.dram_tensor("rw_lw", [2, 128, 128], F32, kind="ExternalInput")
    rw_gw = nc.dram_tensor("rw_gw", [2, 128, 128], F32, kind="ExternalInput")
    sshT = nc.dram_tensor("sshT", [2, 640, NSEQ], F32, kind="ExternalInput")
    swkv = nc.dram_tensor("swkv", [2, NSEQ, 128, 64], F32, kind="ExternalInput")
    sel10 = nc.dram_tensor("sel10", [10, 640], F32, kind="ExternalInput")
    yT_out = nc.dram_tensor("yT_out", [256, T], F32, kind="ExternalOutput")
    kv_out = nc.dram_tensor("kv_out", [2, 128, T], F32, kind="ExternalOutput")
    shift_out = nc.dram_tensor("shift_out", [2, 640, 1 + NSEQ], F32, kind="ExternalOutput")
    pooln_out = nc.dram_tensor("pooln_out", [2, 128, 15 + NS], F32, kind="ExternalOutput")
    poolp_out = nc.dram_tensor("poolp_out", [2, NSEQ, 11, 128], F32, kind="ExternalOutput")
    wkv_out = nc.dram_tensor("wkv_out", [2, 1 + NSEQ, 128, 64], F32, kind="ExternalOutput")
    hb_loc = nc.dram_tensor("hb_loc", [256, T], BF16)
    HALL = nc.dram_tensor("HALL", [D, T], BF16)
    mix_loc = nc.dram_tensor("mix_loc", [256, T], BF16)
    MALL = nc.dram_tensor("MALL", [D, T], BF16)
    hid_loc = nc.dram_tensor("hid_loc", [NFF, T], BF16)
    HIDALL = nc.dram_tensor("HIDALL", [8 * NFF, T], BF16)
    pb = nc.dram_tensor("pb", [2, 256, T], BF16)
    sg = nc.dram_tensor("sg", [256, T], F32)
    U = nc.dram_tensor("U", [NIN, T], F32)
    Vs = nc.dram_tensor("Vs", [NS, 64], BF16)
    OPS = nc.dram_tensor("OPS", [T, 5, 2, 64], F32)

    with contextlib.ExitStack() as stack:
        sems = [stack.enter_context(nc.semaphore(f"s{i}")) for i in range(90)]
        S = Sched(nc, sems)
        C = Ctx(nc, stack)

        hT = C.sb("hT", [128, 2, T], F32)
        b_hT = [Buf("hT0"), Buf("hT1")]
        ones_bf = C.sb("ones_bf", [128, 128], BF16)
        b_ones = Buf("ones")
        S.op("dve", lambda e: e.memset(ones_bf[:, :], 1.0), writes=[b_ones])
        gm = C.sb("gm", [128, 6, KT], F32)
        b_gm = Buf("gm")
        for i in range(6):
            S.dma(lambda e, i=i: e.dma_start(out=gm[:, i, :], in_=gvec[i, :, :]), writes=[b_gm])
        gf = C.sb("gf", [128, 2], F32)
        b_gf = Buf("gf")
        S.dma(lambda e: e.dma_start(out=gf[:, :], in_=gfin[:, :]), writes=[b_gf])
        eps_t = C.sb("eps_t", [128, 1], F32)
        b_eps = Buf("eps")
        S.op("dve", lambda e: e.memset(eps_t[:, :], EPS), writes=[b_eps])
        for ft in range(2):
            S.dma(lambda e, ft=ft: e.dma_start(out=hT[:, ft, :], in_=xT[ft * 128:(ft + 1) * 128, :]),
                  writes=[b_hT[ft]])

        cast_st = [C.sb(f"cast{i}", [128, 2048], BF16) for i in range(2)]
        b_cast = [Buf("cast0"), Buf("cast1")]
        f32_st = [C.sb(f"f32st{i}", [128, 1024], F32) for i in range(2)]
        b_f32 = [Buf("f32st0"), Buf("f32st1")]
        b_hbloc, b_HALL = Buf("hb_loc"), Buf("HALL")
        b_mixloc, b_MALL = Buf("mix_loc"), Buf("MALL")
        b_hidloc, b_HIDALL = Buf("hid_loc"), Buf("HIDALL")
        b_pb, b_sg, b_U = Buf("pb"), Buf("sg"), Buf("U")
        rr = {"cast": 0, "f32": 0, "mm": 0}

        def cast_rows_to_dram(src_tile, b_src, ft, dst, b_dst, row0):
            for c0 in range(0, T, 2048):
                w = min(2048, T - c0)
                j = rr["cast"] % 2
                rr["cast"] += 1
                S.op("act", lambda e, c0=c0, w=w, j=j: e.activation(
                    out=cast_st[j][:, 0:w], in_=src_tile[:, ft, c0:c0 + w], func=AF.Copy),
                    reads=[b_src], writes=[b_cast[j]])
                S.dma(lambda e, c0=c0, w=w, j=j: e.dma_start(
                    out=dst[row0:row0 + 128, c0:c0 + w], in_=cast_st[j][:, 0:w]),
                    reads=[b_cast[j]], writes=[b_dst])

        def allgather(loc, b_loc, glob, b_glob):
            S.dma(lambda e: e.collective_compute(
                "AllGather", ALU.bypass, replica_groups=[list(range(NCORES))],
                ins=[loc.ap().opt()], outs=[glob.ap().opt()]),
                reads=[b_loc], writes=[b_glob], eng="pool", inc=1)

        def allgather_h():
            for ft in range(2):
                cast_rows_to_dram(hT, b_hT[ft], ft, hb_loc, b_hbloc, ft * 128)
            allgather(hb_loc, b_hbloc, HALL, b_HALL)

        WFLAT = KT * 2 * NFF
        Wb = C.sb("Wb", [128, WFLAT], BF16)
        b_Wb = Buf("Wb")

        def load_weight(wap, ktiles, ncols, gidx=None):
            assert ktiles * ncols <= WFLAT
            for kt in range(ktiles):
                for c0 in range(0, ncols, 1024):
                    w = min(1024, ncols - c0)
                    j = rr["f32"] % 2
                    rr["f32"] += 1
                    S.dma(lambda e, kt=kt, c0=c0, w=w, j=j: e.dma_start(
                        out=f32_st[j][:, 0:w], in_=wap[kt * 128:(kt + 1) * 128, c0:c0 + w]),
                        writes=[b_f32[j]])
                    o = kt * ncols + c0
                    if gidx is not None:
                        S.op("dve", lambda e, kt=kt, o=o, w=w, j=j: e.tensor_scalar(
                            out=Wb[:, o:o + w], in0=f32_st[j][:, 0:w],
                            scalar1=gm[:, gidx, kt:kt + 1], scalar2=None, op0=ALU.mult),
                            reads=[b_f32[j], b_gm], writes=[b_Wb])
                    else:
                        S.op("dve", lambda e, o=o, w=w, j=j: e.tensor_copy(
                            out=Wb[:, o:o + w], in_=f32_st[j][:, 0:w]),
                            reads=[b_f32[j]], writes=[b_Wb])

        Xc = [C.sb(f"Xc{i}", [128, XW], BF16) for i in range(2)]
        b_Xc = [Buf("Xc0"), Buf("Xc1")]
        sq = [C.sb(f"sq{i}", [128, 512], BF16) for i in range(2)]
        b_sq = [Buf("sq0"), Buf("sq1")]
        rstd = [C.sb(f"rstd{i}", [128, 512], F32) for i in range(2)]
        b_rstd = [Buf("rstd0"), Buf("rstd1")]
        ps_ss = C.ps("ps_ss", [128, 512])
        b_ps_ss = Buf("ps_ss")
        ps_mm = [C.ps(f"ps_mm{i}", [128, 512]) for i in range(2)]
        b_ps_mm = [Buf("psmm0"), Buf("psmm1")]
        ost = [C.sb(f"ost{i}", [128, 512], F32) for i in range(2)]
        b_ost = [Buf("ost0"), Buf("ost1")]
        tmpf = [C.sb(f"tmpf{i}", [128, 512], F32) for i in range(2)]
        b_tmpf = [Buf("tmpf0"), Buf("tmpf1")]
        tmpb = [C.sb(f"tmpb{i}", [128, 512], BF16) for i in range(2)]
        b_tmpb = [Buf("tmpb0"), Buf("tmpb1")]

        def load_chunk(X, b_X, ktiles, tch, ch, j):
            S.dma(lambda e: e.dma_start(
                out=Xc[j][:, 0:ktiles * tch].rearrange("p (k t) -> p k t", k=ktiles),
                in_=X[:, ch * tch:(ch + 1) * tch].rearrange("(k p) t -> p k t", p=128)),
                reads=[b_X], writes=[b_Xc[j]])

        def chunk_rstd(j):
            for kt in range(KT):
                q = kt % 2
                S.op("act", lambda e, kt=kt, q=q: e.activation(
                    out=sq[q][:, :], in_=Xc[j][:, kt * 512:(kt + 1) * 512], func=AF.Square),
                    reads=[b_Xc[j]], writes=[b_sq[q]])
                S.op("pe", lambda e, kt=kt, q=q: e.matmul(ps_ss[:, :], lhsT=ones_bf[:, :], rhs=sq[q][:, :],
                                                          start=(kt == 0), stop=(kt == KT - 1)),
                     reads=[b_sq[q], b_ones], writes=[b_ps_ss])
            S.op("act", lambda e: e.activation(out=rstd[j][:, :], in_=ps_ss[:, :], func=AF.Sqrt,
                                               bias=eps_t[:, 0:1], scale=1.0 / D),
                 reads=[b_ps_ss, b_eps], writes=[b_rstd[j]])
            S.op("dve", lambda e: e.reciprocal(out=rstd[j][:, :], in_=rstd[j][:, :]),
                 reads=[b_rstd[j]], writes=[b_rstd[j]])

        def gemm(X, b_X, ktiles, tch, ncols, nblocks, sink, norm):
            assert ktiles * tch <= XW
            for ch in range(T // tch):
                j = ch % 2
                load_chunk(X, b_X, ktiles, tch, ch, j)
                if norm:
                    chunk_rstd(j)
                for bi, (n0, nr) in enumerate(nblocks):
                    m = rr["mm"] % 2
                    rr["mm"] += 1
                    for kt in range(ktiles):
                        S.op("pe", lambda e, kt=kt, n0=n0, nr=nr, m=m, j=j: e.matmul(
                            ps_mm[m][0:nr, 0:tch], lhsT=Wb[:, kt * ncols + n0:kt * ncols + n0 + nr],
                            rhs=Xc[j][:, kt * tch:(kt + 1) * tch],
                            start=(kt == 0), stop=(kt == ktiles - 1)),
                            reads=[b_Wb, b_Xc[j]], writes=[b_ps_mm[m]])
                    if norm:
                        S.op("dve", lambda e, nr=nr, m=m, j=j: e.tensor_tensor(
                            out=ost[m][0:nr, 0:tch], in0=ps_mm[m][0:nr, 0:tch], in1=rstd[j][0:nr, 0:tch],
                            op=ALU.mult), reads=[b_ps_mm[m], b_rstd[j]], writes=[b_ost[m]])
                    else:
                        S.op("act", lambda e, nr=nr, m=m: e.activation(
                            out=ost[m][0:nr, 0:tch], in_=ps_mm[m][0:nr, 0:tch], func=AF.Copy),
                            reads=[b_ps_mm[m]], writes=[b_ost[m]])
                    sink(bi, n0, nr, ch, m)

        def blocks128(ncols):
            return [(n0, min(128, ncols - n0)) for n0 in range(0, ncols, 128)]

        for l in range(2):
            for ft in range(2):
                for c0 in range(0, T, 1024):
                    w = min(1024, T - c0)
                    j = rr["f32"] % 2
                    rr["f32"] += 1
                    S.dma(lambda e, l=l, ft=ft, c0=c0, w=w, j=j: e.dma_start(
                        out=f32_st[j][:, 0:w], in_=pT[l, ft * 128:(ft + 1) * 128, c0:c0 + w]),
                        writes=[b_f32[j]])
                    jc = rr["cast"] % 2
                    rr["cast"] += 1
                    S.op("act", lambda e, w=w, j=j, jc=jc: e.activation(
                        out=cast_st[jc][:, 0:w], in_=f32_st[j][:, 0:w], func=AF.Copy),
                        reads=[b_f32[j]], writes=[b_cast[jc]])
                    S.dma(lambda e, l=l, ft=ft, c0=c0, w=w, jc=jc: e.dma_start(
                        out=pb[l, ft * 128:(ft + 1) * 128, c0:c0 + w], in_=cast_st[jc][:, 0:w]),
                        reads=[b_cast[jc]], writes=[b_pb])

        b_out = Buf("outputs")
        ch_last = (SEQ - 1) // 512
        ch_s0 = SEQ // 512

        Pb = [C.ps(f"Pb{i}", [128, 512]) for i in range(5)]
        b_Pb = [Buf(f"Pb{i}") for i in range(5)]
        onesf = C.sb("onesf", [128, 512], F32)
        b_onesf = Buf("onesf")
        S.op("pool", lambda e: e.memset(onesf[:, :], 1.0), writes=[b_onesf])
        ident = C.sb("ident", [128, 128], F32)
        b_ident = Buf("ident")
        S.op("pool", lambda e: e.affine_select(out=ident[:, :], in_=onesf[:, 0:128], pattern=[[-1, 128]],
                                               compare_op=ALU.is_equal, fill=0.0, base=0, channel_multiplier=1),
             reads=[b_onesf], writes=[b_ident])
        one_c = C.sb("one_c", [128, 1], F32)
        b_one = Buf("one_c")
        S.op("pool", lambda e: e.memset(one_c[:, :], 1.0), writes=[b_one])
        small = C.sb("small", [128, 256], F32)
        b_small = Buf("small")
        S.dma(lambda e: e.dma_start(out=small[:, 0:4], in_=selw_in[:, :]), writes=[b_small])
        S.dma(lambda e: e.dma_start(out=small[:, 88:90], in_=sbb_in[:, :]), writes=[b_small])
        S.dma(lambda e: e.dma_start(out=small[0:64, 90:92], in_=psc_in[:, :]), writes=[b_small])
        S.op("pool", lambda e: e.iota(small[:, 72:88], pattern=[[1, 16]], base=1, channel_multiplier=0,
                                      allow_small_or_imprecise_dtypes=True), writes=[b_small])
        for i, wnd in enumerate((2, 4, 8, 16)):
            S.op("dve", lambda e, i=i, wnd=wnd: e.tensor_scalar(
                out=small[:, 8 + 16 * i:24 + 16 * i], in0=small[:, 72:88], scalar1=float(wnd), scalar2=None,
                op0=ALU.min), reads=[b_small], writes=[b_small])
            S.op("dve", lambda e, i=i: e.reciprocal(out=small[:, 8 + 16 * i:24 + 16 * i],
                                                    in_=small[:, 8 + 16 * i:24 + 16 * i]),
                 reads=[b_small], writes=[b_small])
            S.op("dve", lambda e, i=i: e.tensor_scalar(
                out=small[:, 8 + 16 * i:24 + 16 * i], in0=small[:, 8 + 16 * i:24 + 16 * i],
                scalar1=small[:, i:i + 1], scalar2=None, op0=ALU.mult), reads=[b_small], writes=[b_small])
            S.op("dve", lambda e, i=i, wnd=wnd: e.tensor_scalar(
                out=small[:, 4 + i:5 + i], in0=small[:, i:i + 1], scalar1=1.0 / wnd, scalar2=None,
                op0=ALU.mult), reads=[b_small], writes=[b_small])
        smi = C.sb("smi", [128, 8], I32)
        b_smi = Buf("smi")
        S.op("pool", lambda e: e.iota(smi[:, 0:1], pattern=[[0, 1]], base=0, channel_multiplier=1),
             writes=[b_smi])
        S.op("pool", lambda e: e.iota(smi[:, 2:3], pattern=[[0, 1]], base=3, channel_multiplier=0),
             writes=[b_smi])
        S.op("dve", lambda e: e.tensor_scalar(out=smi[:, 1:2], in0=smi[:, 0:1], scalar1=smi[:, 2:3],
                                              scalar2=None, op0=ALU.bitwise_and), reads=[b_smi], writes=[b_smi])
        S.op("dve", lambda e: e.tensor_copy(out=small[:, 134:135], in_=smi[:, 1:2]), reads=[b_smi],
             writes=[b_small])
        S.op("pool", lambda e: e.iota(small[:, 136:140], pattern=[[1, 4]], base=0, channel_multiplier=0,
                                      allow_small_or_imprecise_dtypes=True), writes=[b_small])
        S.op("dve", lambda e: e.tensor_scalar(out=small[:, 130:134], in0=small[:, 136:140],
                                              scalar1=small[:, 134:135], scalar2=None, op0=ALU.is_lt),
             reads=[b_small], writes=[b_small])
        wplb = C.sb("wplb", [128, 2, 64], BF16)
        b_wplb = Buf("wplb")
        for l in range(2):
            S.dma(lambda e, l=l: e.dma_start(out=f32_st[l][:, 0:64], in_=wpool_in[l, :, :]), writes=[b_f32[l]])
            S.op("dve", lambda e, l=l: e.tensor_copy(out=wplb[:, l, :], in_=f32_st[l][:, 0:64]),
                 reads=[b_f32[l]], writes=[b_wplb])
        IDX = C.sb("IDX", [128, NSEQ * NPG], I32)
        b_IDX = Buf("IDX")
        S.dma(lambda e: e.dma_start(out=IDX[:, :], in_=ptab[0:1, :].partition_broadcast(128)), writes=[b_IDX])
        S.op("dve", lambda e: e.tensor_scalar(out=IDX[:, :], in0=IDX[:, :], scalar1=128, scalar2=smi[:, 0:1],
                                              op0=ALU.mult, op1=ALU.add), reads=[b_IDX, b_smi], writes=[b_IDX])

        X1f = Xc[1][:, :].bitcast(F32)
        pE = [X1f[:, 0:640], X1f[:, 640:1280]]
        pS = [X1f[:, 1280 + 640 * i:1920 + 640 * i] for i in range(4)]
        b_pE = [Buf("pE0"), Buf("pE1")]
        b_pS = [Buf(f"pS{i}") for i in range(4)]

        def mixer_pool(l):
            def window_sums(Et, bE, n, sl):
                prev, bprev = Et, bE
                for i, sh in enumerate((1, 2, 4, 8)):
                    lo = 2 * sh - 1
                    S.op("dve", lambda e, i=i, sh=sh, lo=lo, prev=prev: e.tensor_tensor(
                        out=sl(pS[i], lo, n), in0=sl(prev, lo, n), in1=sl(prev, lo - sh, n - sh), op=ALU.add),
                        reads=[bprev], writes=[b_pS[i]])
                    prev, bprev = pS[i], b_pS[i]

            def combine(acc_ap, b_acc, n, sl, Et, bE, out_ap, b_o, first16):
                S.op("dve", lambda e: e.tensor_scalar(out=acc_ap, in0=sl(pS[0], 15, n), scalar1=small[:, 4:5],
                                                      scalar2=None, op0=ALU.mult),
                     reads=[b_pS[0], b_small], writes=[b_acc])
                for i in range(1, 4):
                    S.op("dve", lambda e, i=i: e.scalar_tensor_tensor(
                        out=acc_ap, in0=sl(pS[i], 15, n), scalar=small[:, 4 + i:5 + i], in1=acc_ap,
                        op0=ALU.mult, op1=ALU.add), reads=[b_pS[i], b_small, b_acc], writes=[b_acc])
                if first16:
                    t16 = tmpf[1][:, 0:16]
                    u16 = tmpf[1][:, 16:32]
                    S.op("dve", lambda e: e.tensor_tensor(out=t16, in0=pS[0][:, 15:31], in1=small[:, 8:24],
                                                          op=ALU.mult), reads=[b_pS[0], b_small],
                         writes=[b_tmpf[1]])
                    for i in range(1, 4):
                        S.op("dve", lambda e, i=i: e.tensor_tensor(
                            out=u16, in0=pS[i][:, 15:31], in1=small[:, 8 + 16 * i:24 + 16 * i], op=ALU.mult),
                            reads=[b_pS[i], b_small, b_tmpf[1]], writes=[b_tmpf[1]])
                        S.op("dve", lambda e: e.tensor_tensor(out=t16, in0=t16, in1=u16, op=ALU.add),
                             reads=[b_tmpf[1]], writes=[b_tmpf[1]])
                    S.op("dve", lambda e: e.tensor_copy(out=acc_ap[:, 0:16], in_=t16), reads=[b_tmpf[1], b_acc],
                         writes=[b_acc])
                S.op("dve", lambda e: e.tensor_tensor(out=out_ap, in0=acc_ap, in1=sl(Et, 15, n), op=ALU.subtract),
                     reads=[b_acc, bE], writes=[b_o])

            def project(l, cs, kq):
                m = rr["mm"] % 2
                rr["mm"] += 1
                S.op("pe", lambda e: e.matmul(ps_mm[m][0:64, :], lhsT=wplb[:, l, :], rhs=tmpb[kq][:, :],
                                              start=True, stop=True),
                     reads=[b_wplb, b_tmpb[kq]], writes=[b_ps_mm[m]])
                S.op("dve", lambda e: e.tensor_scalar(out=cast_st[1][0:64, 0:512], in0=ps_mm[m][0:64, :],
                                                      scalar1=small[0:64, 90 + l:91 + l], scalar2=None,
                                                      op0=ALU.mult),
                     reads=[b_ps_mm[m], b_small], writes=[b_cast[1]])
                S.dma(lambda e: e.dma_start(out=mix_loc[192:256, cs], in_=cast_st[1][0:64, 0:512]),
                      reads=[b_cast[1]], writes=[b_mixloc])

            sl2 = lambda t, a, b: t[:, a:b]
            for ch in range(SEQ // 512):
                j = ch % 2
                Ej = pE[j]
                if ch == 0:
                    S.op("pool", lambda e, Ej=Ej: e.memset(Ej[:, 0:15], 0.0), writes=[b_pE[j]])
                    S.dma(lambda e, Ej=Ej: e.dma_start(out=Ej[:, 15:527], in_=U[896:1024, 0:512]),
                          reads=[b_U], writes=[b_pE[j]])
                else:
                    S.dma(lambda e, Ej=Ej, ch=ch: e.dma_start(out=Ej[:, 0:527],
                                                              in_=U[896:1024, ch * 512 - 15:ch * 512 + 512]),
                          reads=[b_U], writes=[b_pE[j]])
                window_sums(Ej, b_pE[j], 527, sl2)
                combine(tmpf[0][:, :], b_tmpf[0], 527, sl2, Ej, b_pE[j], tmpb[j][:, :], b_tmpb[j], ch == 0)
                project(l, slice(ch * 512, (ch + 1) * 512), j)
            sl3 = lambda t, a, b: t[:, 0:32 * 19].rearrange("p (b j) -> p b j", j=19)[:, :, a:b]
            for sb_ in range(NSEQ // 32):
                j = sb_ % 2
                Ej = pE[j]
                E3 = Ej[:, 0:32 * 19].rearrange("p (b j) -> p b j", j=19)
                S.dma(lambda e, E3=E3, sb_=sb_: e.dma_start(out=E3[:, :, 0:15],
                                                            in_=spT[l, :, sb_ * 32:(sb_ + 1) * 32, :]),
                      writes=[b_pE[j]])
                S.dma(lambda e, E3=E3, sb_=sb_: e.dma_start(
                    out=E3[:, :, 15:19],
                    in_=U[896:1024, SEQ + sb_ * 128:SEQ + (sb_ + 1) * 128].rearrange("p (b t) -> p b t", t=4)),
                    reads=[b_U], writes=[b_pE[j]])
                window_sums(Ej, b_pE[j], 19, sl3)
                acc3 = tmpf[0][:, 0:128].rearrange("p (b t) -> p b t", t=4)
                out3 = tmpb[0][:, sb_ * 128:(sb_ + 1) * 128].rearrange("p (b t) -> p b t", t=4)
                combine(acc3, b_tmpf[0], 19, sl3, Ej, b_pE[j], out3, b_tmpb[0], False)
            project(l, slice(SEQ, SEQ + 512), 0)

        NT = T // 128
        QT = Wb[0:64, 0:T]
        KTt = Wb[0:64, T:2 * T]
        Vt = Wb[:, 2 * T:2 * T + NT * 64]
        b_QT, b_KT, b_V = Buf("QT"), Buf("KT"), Buf("V")
        Qm = Xc[0][0:64, 0:32 * 128]
        b_Qm = Buf("Qm")
        KTb = [Xc[0][0:64, 4096:4608], Xc[0][0:64, 4608:5120]]
        b_KTb = [Buf("KTb0"), Buf("KTb1")]
        attT = Xc[0][:, 5120:5120 + 17 * 128]
        b_attT = Buf("attT")
        VN = cast_st[1][0:4, 0:32 * 64]
        b_VN = b_cast[1]
        Zs = C.sb("Zs", [128, NPG * 128 + 4], F32)
        b_Zs = Buf("Zs")
        Kpg = [C.sb(f"Kpg{i}", [128, 4 * 64], F32) for i in range(2)]
        b_Kpg = [Buf("Kpg0"), Buf("Kpg1")]
        Vpg = [C.sb(f"Vpg{i}", [128, 64], F32) for i in range(2)]
        b_Vpg = [Buf("Vpg0"), Buf("Vpg1")]
        Vpb = [C.sb(f"Vpb{i}", [128, 64], BF16) for i in range(2)]
        b_Vpb = [Buf("Vpb0"), Buf("Vpb1")]
        b_Vs = Buf("Vs")
        Ech, Lch, T1, ATT = ost[0], ost[1], rstd[0], rstd[1]
        b_Ech, b_Lch, b_T1, b_ATT = b_ost[0], b_ost[1], b_rstd[0], b_rstd[1]
        CSt, b_CS = tmpf, b_tmpf
        zi = {"z": 0, "t": 0, "k": 0, "v": 0}

        def sb_rows(l, widths, zsrc, mask, emit_av):
            nch = len(widths)
            sbb_l = small[:, 88 + l:89 + l]

            def chunk_EL(ci):
                w = widths[ci]
                zap, bz = zsrc(ci)
                S.op("act", lambda e: e.activation(out=Ech[:, 0:w], in_=zap, func=AF.Exp, bias=sbb_l, scale=1.0),
                     reads=[bz, b_small], writes=[b_Ech])
                S.op("act", lambda e: e.activation(out=Lch[:, 0:w], in_=Ech[:, 0:w], func=AF.Ln,
                                                   bias=one_c[:, 0:1], scale=1.0),
                     reads=[b_Ech, b_one], writes=[b_Lch])
                if ci == nch - 1:
                    mask(Lch, b_Lch, w)
                return w, zap, bz

            for ci in range(nch):
                w, zap, bz = chunk_EL(ci)
                S.op("dve", lambda e, ci=ci, w=w: e.tensor_reduce(
                    out=small[:, 96 + ci:97 + ci], in_=Lch[:, 0:w], axis=mybir.AxisListType.X, op=ALU.add),
                    reads=[b_Lch], writes=[b_small])
            S.op("dve", lambda e: e.tensor_reduce(out=small[:, 128:129], in_=small[:, 96:96 + nch],
                                                  axis=mybir.AxisListType.X, op=ALU.add),
                 reads=[b_small], writes=[b_small])
            S.op("dve", lambda e: e.tensor_tensor(out=small[:, 129:130], in0=sbb_l, in1=small[:, 128:129],
                                                  op=ALU.subtract), reads=[b_small], writes=[b_small])
            wprev = None
            for ci in range(nch):
                w, zap, bz = chunk_EL(ci)
                cj = ci % 2
                init = 0.0 if ci == 0 else CSt[1 - cj][:, wprev - 1:wprev]
                S.op("dve", lambda e, w=w, cj=cj, init=init: e.tensor_tensor_scan(
                    out=CSt[cj][:, 0:w], data0=onesf[:, 0:w], data1=Lch[:, 0:w], initial=init,
                    op0=ALU.mult, op1=ALU.add),
                    reads=[b_onesf, b_Lch, b_CS[1 - cj]], writes=[b_CS[cj]])
                S.op("dve", lambda e, w=w, zap=zap: e.tensor_tensor(out=T1[:, 0:w], in0=zap, in1=Lch[:, 0:w],
                                                                    op=ALU.subtract),
                     reads=[bz, b_Lch], writes=[b_T1])
                S.op("dve", lambda e, w=w, cj=cj: e.tensor_tensor(out=T1[:, 0:w], in0=T1[:, 0:w],
                                                                   in1=CSt[cj][:, 0:w], op=ALU.add),
                     reads=[b_T1, b_CS[cj]], writes=[b_T1])
                S.op("act", lambda e, w=w: e.activation(out=ATT[:, 0:w], in_=T1[:, 0:w], func=AF.Exp,
                                                        bias=small[:, 129:130], scale=1.0),
                     reads=[b_T1, b_small], writes=[b_ATT])
                if ci == nch - 1:
                    mask(ATT, b_ATT, w)
                emit_av(ci, w)
                wprev = w

        def transpose_to(dst_ap, b_dst, src_ap, b_src, rows, cols, scale=None):
            t = 3 + zi["t"] % 2
            zi["t"] += 1
            S.op("pe", lambda e: e.transpose(out=Pb[t][0:cols, 0:rows], in_=src_ap,
                                             identity=ident[0:rows, 0:rows]),
                 reads=[b_src, b_ident], writes=[b_Pb[t]])
            S.op("act", lambda e: e.activation(out=dst_ap, in_=Pb[t][0:cols, 0:rows], func=AF.Copy),
                 reads=[b_Pb[t]], writes=[b_dst])

        def mixer_sb(l):
            for c0 in range(0, T, 1024):
                w = min(1024, T - c0)
                for (r0, dst, bd, sc) in ((640, QT, b_QT, 0.125), (704, KTt, b_KT, 1.0)):
                    j = rr["f32"] % 2
                    rr["f32"] += 1
                    S.dma(lambda e, j=j, r0=r0, c0=c0, w=w: e.dma_start(out=f32_st[j][0:64, 0:w],
                                                                        in_=U[r0:r0 + 64, c0:c0 + w]),
                          reads=[b_U], writes=[b_f32[j]])
                    S.op("act", lambda e, j=j, dst=dst, c0=c0, w=w, sc=sc: e.activation(
                        out=dst[:, c0:c0 + w], in_=f32_st[j][0:64, 0:w], func=AF.Copy, scale=sc),
                        reads=[b_f32[j]], writes=[bd])
                j = rr["f32"] % 2
                rr["f32"] += 1
                S.dma(lambda e, j=j, c0=c0, w=w: e.dma_start(out=f32_st[j][0:64, 0:w], in_=U[768:832, c0:c0 + w]),
                      reads=[b_U], writes=[b_f32[j]])
                for bi in range(w // 128):
                    tt = c0 // 128 + bi
                    transpose_to(Vt[:, tt * 64:(tt + 1) * 64], b_V, f32_st[j][0:64, bi * 128:(bi + 1) * 128],
                                 b_f32[j], 64, 128)
            for i in range(NS // 128):
                tt = SEQ // 128 + i
                S.dma(lambda e, i=i, tt=tt: e.dma_start(out=Vs[i * 128:(i + 1) * 128, :],
                                                        in_=Vt[:, tt * 64:(tt + 1) * 64]),
                      reads=[b_V], writes=[b_Vs])

            def mask_diag(tile, bt, w):
                S.op("pool", lambda e: e.affine_select(out=tile[:, w - 128:w], in_=tile[:, w - 128:w],
                                                       pattern=[[-1, 128]], compare_op=ALU.is_gt, fill=0.0,
                                                       base=0, channel_multiplier=1),
                     reads=[bt], writes=[bt])

            for qt in range(SEQ // 128):
                nk = 128 * (qt + 1)
                widths = [512] * (nk // 512) + ([nk % 512] if nk % 512 else [])
                nkt = qt + 1

                def zsrc(ci, qt=qt, widths=widths):
                    w = widths[ci]
                    z = 1 + zi["z"] % 2
                    zi["z"] += 1
                    S.op("pe", lambda e: e.matmul(Pb[z][:, 0:w], lhsT=QT[:, qt * 128:(qt + 1) * 128],
                                                  rhs=KTt[:, ci * 512:ci * 512 + w], start=True, stop=True),
                         reads=[b_QT, b_KT], writes=[b_Pb[z]])
                    return Pb[z][:, 0:w], b_Pb[z]

                def emit_av(ci, w, qt=qt, nkt=nkt):
                    for kb in range(w // 128):
                        kt = ci * 4 + kb
                        a = zi["k"] % 2
                        zi["k"] += 1
                        transpose_to(tmpb[a][:, 0:128], b_tmpb[a], ATT[:, kb * 128:(kb + 1) * 128], b_ATT, 128, 128)
                        S.op("pe", lambda e, kt=kt, a=a: e.matmul(
                            Pb[0][0:64, 0:128], lhsT=Vt[:, kt * 64:(kt + 1) * 64], rhs=tmpb[a][:, 0:128],
                            start=(kt == 0), stop=(kt == nkt - 1)),
                            reads=[b_V, b_tmpb[a]], writes=[b_Pb[0]])
                sb_rows(l, widths, zsrc, mask_diag, emit_av)
                S.op("dve", lambda e: e.tensor_copy(out=cast_st[0][0:64, 0:128], in_=Pb[0][0:64, 0:128]),
                     reads=[b_Pb[0]], writes=[b_cast[0]])
                S.dma(lambda e, qt=qt: e.dma_start(out=mix_loc[128:192, qt * 128:(qt + 1) * 128],
                                                   in_=cast_st[0][0:64, 0:128]),
                      reads=[b_cast[0]], writes=[b_mixloc])

            PASTK = NPG * 128
            pchunks = [(p0, min(4, NPG - p0)) for p0 in range(0, NPG, 4)]
            widths_s = [128 * n for (_, n) in pchunks] + [4]

            def mask_new(tile, bt, w):
                S.op("dve", lambda e: e.tensor_tensor(out=tile[:, 0:4], in0=tile[:, 0:4], in1=small[:, 130:134],
                                                      op=ALU.mult), reads=[bt, b_small], writes=[bt])

            for g in range(NSEQ // 32):
                r0 = SEQ + g * 128
                S.op("pool", lambda e: e.memset(Qm, 0.0), writes=[b_Qm])
                for b in range(32):
                    S.op("pool", lambda e, b=b, r0=r0: e.tensor_copy(
                        out=Qm[:, b * 128 + 4 * b:b * 128 + 4 * b + 4], in_=QT[:, r0 + 4 * b:r0 + 4 * b + 4]),
                        reads=[b_QT], writes=[b_Qm])
                for ci, (p0, npg) in enumerate(pchunks):
                    w = 128 * npg
                    for b in range(32):
                        sq_ = g * 32 + b
                        kq = zi["k"] % 2
                        zi["k"] += 1
                        for pi in range(npg):
                            col = sq_ * NPG + p0 + pi
                            S.dma(lambda e, kq=kq, pi=pi, col=col: e.indirect_dma_start(
                                out=Kpg[kq][:, pi * 64:(pi + 1) * 64], out_offset=None, in_=ck[l][:, :],
                                in_offset=bass.IndirectOffsetOnAxis(ap=IDX[:, col:col + 1], axis=0)),
                                reads=[b_IDX], writes=[b_Kpg[kq]], eng="pool")
                        for pi in range(npg):
                            transpose_to(KTb[kq][:, pi * 128:(pi + 1) * 128], b_KTb[kq],
                                         Kpg[kq][:, pi * 64:(pi + 1) * 64], b_Kpg[kq], 128, 64)
                        S.op("pe", lambda e, b=b, kq=kq, w=w: e.matmul(
                            Pb[1][:, 0:w], lhsT=Qm[:, b * 128:(b + 1) * 128], rhs=KTb[kq][:, 0:w],
                            start=(b == 0), stop=(b == 31)),
                            reads=[b_Qm, b_KTb[kq]], writes=[b_Pb[1]])
                    S.op("dve", lambda e, p0=p0, w=w: e.tensor_copy(out=Zs[:, p0 * 128:p0 * 128 + w],
                                                                    in_=Pb[1][:, 0:w]),
                         reads=[b_Pb[1]], writes=[b_Zs])
                for b in range(32):
                    S.op("pe", lambda e, b=b, r0=r0: e.matmul(
                        Pb[2][:, 0:4], lhsT=Qm[:, b * 128:(b + 1) * 128], rhs=KTt[:, r0 + 4 * b:r0 + 4 * b + 4],
                        start=(b == 0), stop=(b == 31)),
                        reads=[b_Qm, b_KT], writes=[b_Pb[2]])
                S.op("dve", lambda e: e.tensor_copy(out=Zs[:, PASTK:PASTK + 4], in_=Pb[2][:, 0:4]),
                     reads=[b_Pb[2]], writes=[b_Zs])
                S.dma(lambda e, g=g: e.dma_start(
                    out=VN.rearrange("j (b d) -> j b d", d=64),
                    in_=Vs[g * 128:(g + 1) * 128, :].rearrange("(b j) d -> j b d", j=4)),
                    reads=[b_Vs], writes=[b_VN])

                def zsrc_s(ci):
                    o = pchunks[ci][0] * 128 if ci < len(pchunks) else PASTK
                    return Zs[:, o:o + widths_s[ci]], b_Zs

                def emit_av_s(ci, w):
                    if ci < len(pchunks):
                        for kb in range(w // 128):
                            kt = pchunks[ci][0] + kb
                            transpose_to(attT[:, kt * 128:(kt + 1) * 128], b_attT,
                                         ATT[:, kb * 128:(kb + 1) * 128], b_ATT, 128, 128)
                    else:
                        transpose_to(attT[0:4, NPG * 128:(NPG + 1) * 128], b_attT, ATT[:, 0:4], b_ATT, 128, 4)
                sb_rows(l, widths_s, zsrc_s, mask_new, emit_av_s)
                for b in range(32):
                    sq_ = g * 32 + b
                    for kt in range(NPG):
                        vq = zi["v"] % 2
                        zi["v"] += 1
                        col = sq_ * NPG + kt
                        S.dma(lambda e, vq=vq, col=col: e.indirect_dma_start(
                            out=Vpg[vq][:, :], out_offset=None, in_=cv[l][:, :],
                            in_offset=bass.IndirectOffsetOnAxis(ap=IDX[:, col:col + 1], axis=0)),
                            reads=[b_IDX], writes=[b_Vpg[vq]], eng="pool")
                        S.op("dve", lambda e, vq=vq: e.tensor_copy(out=Vpb[vq][:, :], in_=Vpg[vq][:, :]),
                             reads=[b_Vpg[vq]], writes=[b_Vpb[vq]])
                        S.op("pe", lambda e, vq=vq, kt=kt, b=b: e.matmul(
                            Pb[0][0:64, 4 * b:4 * b + 4], lhsT=Vpb[vq][:, :],
                            rhs=attT[:, kt * 128 + 4 * b:kt * 128 + 4 * b + 4], start=(kt == 0), stop=False),
                            reads=[b_Vpb[vq], b_attT], writes=[b_Pb[0]])
                    S.op("pe", lambda e, b=b: e.matmul(
                        Pb[0][0:64, 4 * b:4 * b + 4], lhsT=VN[:, b * 64:(b + 1) * 64],
                        rhs=attT[0:4, NPG * 128 + 4 * b:NPG * 128 + 4 * b + 4], start=False, stop=True),
                        reads=[b_VN, b_attT], writes=[b_Pb[0]])
                S.op("dve", lambda e: e.tensor_copy(out=cast_st[0][0:64, 0:128], in_=Pb[0][0:64, 0:128]),
                     reads=[b_Pb[0]], writes=[b_cast[0]])
                S.dma(lambda e, r0=r0: e.dma_start(out=mix_loc[128:192, r0:r0 + 128], in_=cast_st[0][0:64, 0:128]),
                      reads=[b_cast[0]], writes=[b_mixloc])

        Wf = Wb[:, :].bitcast(F32)
        NSL = 21
        slot = [Wf[:, i * 520:(i + 1) * 520] for i in range(NSL)]
        b_sl = [Buf(f"slot{i}") for i in range(NSL)]
        X1b = Xc[1]
        th_bf, sg_bf, o_bf = X1b[:, 0:512], X1b[:, 512:1024], X1b[:, 1024:1536]
        b_th, b_sgb, b_ob = Buf("th_bf"), Buf("sg_bf"), Buf("o_bf")
        X0f = Xc[0][:, :].bitcast(F32)
        X2all = [X0f[0:10, 0:2048], X0f[0:10, 2048:4096]]
        b_X2 = [Buf("X2a"), Buf("X2b")]
        b_OPS = Buf("OPS")
        rwc = C.sb("rwc", [128, 32], F32)
        b_rwc = Buf("rwc")
        S.dma(lambda e: e.dma_start(out=rwc[:, 0:10], in_=rw_mu[:, :]), writes=[b_rwc])
        S.dma(lambda e: e.dma_start(out=rwc[:, 10:24], in_=rw_prm[:, :]), writes=[b_rwc])
        S.op("pool", lambda e: e.memset(rwc[:, 24:25], 64e-5), writes=[b_rwc])
        LWb = C.sb("LWb", [128, 2, 128], BF16)
        GWb = C.sb("GWb", [128, 2, 128], BF16)
        b_LW, b_GW = Buf("LWb"), Buf("GWb")
        for l in range(2):
            S.dma(lambda e, l=l: e.dma_start(out=f32_st[0][:, 0:128], in_=rw_lw[l, :, :]), writes=[b_f32[0]])
            S.op("dve", lambda e, l=l: e.tensor_copy(out=LWb[:, l, :], in_=f32_st[0][:, 0:128]),
                 reads=[b_f32[0]], writes=[b_LW])
            S.dma(lambda e, l=l: e.dma_start(out=f32_st[1][:, 0:128], in_=rw_gw[l, :, :]), writes=[b_f32[1]])
            S.op("dve", lambda e, l=l: e.tensor_copy(out=GWb[:, l, :], in_=f32_st[1][:, 0:128]),
                 reads=[b_f32[1]], writes=[b_GW])
        BO = C.sb("BO", [128, 128], F32)
        b_BO = Buf("BO")
        S.op("pool", lambda e: e.memset(BO[:, :], 0.0), writes=[b_BO])
        S.op("pool", lambda e: e.memset(BO[0:64, 0:64], 1.0), writes=[b_BO])
        S.op("pool", lambda e: e.memset(BO[64:128, 64:128], 1.0), writes=[b_BO])
        selt = C.sb("selt", [10, 640], F32)
        b_sel = Buf("selt")
        S.dma(lambda e: e.dma_start(out=selt[:, :], in_=sel10[:, :]), writes=[b_sel])
        Sst = C.sb("Sst", [128, 64], F32)
        Tsc = C.sb("Tsc", [128, 64], F32)
        S3, T3 = tmpf[1], tmpf[0]
        sat = C.sb("sat", [128, 16], F32)
        b_S, b_Tsc, b_S3, b_T3, b_sa = Buf("Sst"), Buf("Tsc"), b_tmpf[1], b_tmpf[0], Buf("sat")
        b_bank = [[b] * 2 for b in (Buf(f"bank{o}") for o in range(5))]
        b_wkv = b_out

        def mixer_rwkv(l):
            mu = lambda blk: rwc[:, l * 5 + blk:l * 5 + blk + 1]
            prm = lambda i: rwc[:, 10 + l * 7 + i:11 + l * 7 + i]
            UH, XS = slot[0:5], slot[5:10]
            bUH, bXS = b_sl[0:5], b_sl[5:10]
            DEC, AA, KKN, KMOD, NA_, TA, BON, GT, YT, TB, TC = slot[10:21]
            bDEC, bAA, bKKN, bKMOD, bNA, bTA, bBON, bGT, bYT, bTB, bTC = b_sl[10:21]

            def bo_sum(src, bsrc):
                m = rr["mm"] % 2
                rr["mm"] += 1
                S.op("pe", lambda e: e.matmul(ps_mm[m][:, :], lhsT=BO[:, :], rhs=src, start=True, stop=True),
                     reads=[b_BO, bsrc], writes=[b_ps_mm[m]])
                return ps_mm[m], b_ps_mm[m]

            def prep(c0, sample):
                for blk in range(5):
                    rows = slice(blk * 128, (blk + 1) * 128)
                    uh, xs = UH[blk], XS[blk]
                    if not sample:
                        if c0 == 0:
                            S.op("pool", lambda e, uh=uh: e.memset(uh[:, 0:1], 0.0), writes=[bUH[blk]])
                            S.dma(lambda e, uh=uh, rows=rows: e.dma_start(out=uh[:, 1:513], in_=U[rows, 0:512]),
                                  reads=[b_U], writes=[bUH[blk]])
                        else:
                            S.dma(lambda e, uh=uh, rows=rows: e.dma_start(out=uh[:, 0:513],
                                                                          in_=U[rows, c0 - 1:c0 + 512]),
                                  reads=[b_U], writes=[bUH[blk]])
                        S.op("dve", lambda e, uh=uh, xs=xs: e.tensor_tensor(out=xs[:, 0:512], in0=uh[:, 0:512],
                                                                            in1=uh[:, 1:513], op=ALU.subtract),
                             reads=[bUH[blk]], writes=[bXS[blk]])
                    else:
                        S.dma(lambda e, uh=uh, rows=rows: e.dma_start(out=uh[:, 1:513], in_=U[rows, c0:c0 + 512]),
                              reads=[b_U], writes=[bUH[blk]])
                        S.dma(lambda e, rows=rows: e.dma_start(out=TB[:, 0:NSEQ], in_=sshT[l, rows, :]),
                              writes=[bTB])
                        sh3 = TA[:, 0:512].rearrange("p (b t) -> p b t", t=4)
                        u3 = uh[:, 1:513].rearrange("p (b t) -> p b t", t=4)
                        S.op("dve", lambda e, sh3=sh3, u3=u3: e.tensor_copy(out=sh3[:, :, 1:4], in_=u3[:, :, 0:3]),
                             reads=[bUH[blk]], writes=[bTA])
                        S.op("dve", lambda e, sh3=sh3: e.tensor_copy(out=sh3[:, :, 0:1],
                                                                     in_=TB[:, 0:NSEQ].unsqueeze(2)),
                             reads=[bTB], writes=[bTA])
                        S.op("dve", lambda e, uh=uh, xs=xs: e.tensor_tensor(out=xs[:, 0:512], in0=TA[:, 0:512],
                                                                            in1=uh[:, 1:513], op=ALU.subtract),
                             reads=[bTA, bUH[blk]], writes=[bXS[blk]])
                    S.op("dve", lambda e, uh=uh, xs=xs, blk=blk: e.scalar_tensor_tensor(
                        out=xs[:, 0:512], in0=xs[:, 0:512], scalar=mu(blk), in1=uh[:, 1:513],
                        op0=ALU.mult, op1=ALU.add), reads=[bXS[blk], bUH[blk], b_rwc], writes=[bXS[blk]])
                XR, XK, XV, XWA, XG = [x[:, 0:512] for x in XS]
                S.op("act", lambda e: e.activation(out=th_bf[0:64, :], in_=XWA[0:64, :], func=AF.Tanh),
                     reads=[bXS[3]], writes=[b_th])
                S.op("act", lambda e: e.activation(out=th_bf[64:128, :], in_=XWA[64:128, :], func=AF.Copy),
                     reads=[bXS[3]], writes=[b_th])
                m = rr["mm"] % 2
                rr["mm"] += 1
                S.op("pe", lambda e, m=m: e.matmul(ps_mm[m][:, :], lhsT=LWb[0:64, l, :], rhs=th_bf[0:64, :],
                                              start=True, stop=True), reads=[b_LW, b_th], writes=[b_ps_mm[m]])
                S.op("act", lambda e, m=m: e.activation(out=TA[:, 0:512], in_=ps_mm[m][:, :], func=AF.Sigmoid,
                                                   bias=prm(0), scale=1.0),
                     reads=[b_ps_mm[m], b_rwc], writes=[bTA])
                S.op("act", lambda e: e.activation(out=DEC[:, 0:512], in_=TA[:, 0:512], func=AF.Exp,
                                                   scale=-0.6065306597126334), reads=[bTA], writes=[bDEC])
                m = rr["mm"] % 2
                rr["mm"] += 1
                S.op("pe", lambda e, m=m: e.matmul(ps_mm[m][:, :], lhsT=LWb[64:128, l, :], rhs=th_bf[64:128, :],
                                              start=True, stop=True), reads=[b_LW, b_th], writes=[b_ps_mm[m]])
                S.op("act", lambda e, m=m: e.activation(out=AA[:, 0:512], in_=ps_mm[m][:, :], func=AF.Sigmoid,
                                                   bias=prm(1), scale=1.0),
                     reads=[b_ps_mm[m], b_rwc], writes=[bAA])
                S.op("act", lambda e: e.activation(out=sg_bf[:, :], in_=XG, func=AF.Sigmoid),
                     reads=[bXS[4]], writes=[b_sgb])
                m = rr["mm"] % 2
                rr["mm"] += 1
                S.op("pe", lambda e, m=m: e.matmul(ps_mm[m][:, :], lhsT=GWb[:, l, :], rhs=sg_bf[:, :],
                                              start=True, stop=True), reads=[b_GW, b_sgb], writes=[b_ps_mm[m]])
                S.op("act", lambda e, m=m: e.activation(out=GT[:, 0:512], in_=ps_mm[m][:, :], func=AF.Copy),
                     reads=[b_ps_mm[m]], writes=[bGT])
                S.op("dve", lambda e: e.tensor_scalar(out=KKN[:, 0:512], in0=XK, scalar1=prm(2), scalar2=None,
                                                      op0=ALU.mult), reads=[bXS[1], b_rwc], writes=[bKKN])
                S.op("dve", lambda e: e.tensor_tensor(out=TA[:, 0:512], in0=KKN[:, 0:512], in1=KKN[:, 0:512],
                                                      op=ALU.mult), reads=[bKKN], writes=[bTA])
                ps, bps = bo_sum(TA[:, 0:512], bTA)
                S.op("act", lambda e, ps=ps: e.activation(out=TA[:, 0:512], in_=ps[:, :], func=AF.Sqrt),
                     reads=[bps], writes=[bTA])
                S.op("dve", lambda e: e.tensor_scalar(out=TA[:, 0:512], in0=TA[:, 0:512], scalar1=1e-12,
                                                      scalar2=None, op0=ALU.max), reads=[bTA], writes=[bTA])
                S.op("dve", lambda e: e.reciprocal(out=TA[:, 0:512], in_=TA[:, 0:512]), reads=[bTA], writes=[bTA])
                S.op("dve", lambda e: e.tensor_tensor(out=KKN[:, 0:512], in0=KKN[:, 0:512], in1=TA[:, 0:512],
                                                      op=ALU.mult), reads=[bKKN, bTA], writes=[bKKN])
                S.op("dve", lambda e: e.tensor_scalar(out=TA[:, 0:512], in0=AA[:, 0:512], scalar1=-1.0,
                                                      scalar2=prm(3), op0=ALU.add, op1=ALU.mult),
                     reads=[bAA, b_rwc], writes=[bTA])
                S.op("dve", lambda e: e.scalar_tensor_tensor(out=KMOD[:, 0:512], in0=TA[:, 0:512], scalar=1.0,
                                                             in1=XK, op0=ALU.add, op1=ALU.mult),
                     reads=[bTA, bXS[1]], writes=[bKMOD])
                S.op("dve", lambda e: e.scalar_tensor_tensor(out=NA_[:, 0:512], in0=KKN[:, 0:512], scalar=-1.0,
                                                             in1=AA[:, 0:512], op0=ALU.mult, op1=ALU.mult),
                     reads=[bKKN, bAA], writes=[bNA])
                S.op("dve", lambda e: e.scalar_tensor_tensor(out=TA[:, 0:512], in0=XR, scalar=prm(4),
                                                             in1=KMOD[:, 0:512], op0=ALU.mult, op1=ALU.mult),
                     reads=[bXS[0], bKMOD, b_rwc], writes=[bTA])
                ps, bps = bo_sum(TA[:, 0:512], bTA)
                S.op("dve", lambda e, ps=ps: e.tensor_tensor(out=BON[:, 0:512], in0=ps[:, :], in1=XV, op=ALU.mult),
                     reads=[bps, bXS[2]], writes=[bBON])
                for oi, (src, bsrc) in enumerate(((KKN, bKKN), (DEC, bDEC), (NA_, bNA), (KMOD, bKMOD),
                                                  (XS[0], bXS[0]))):
                    st, bst = (TB, bTB) if oi % 2 == 0 else (TC, bTC)
                    for bi in range(4):
                        transpose_to(st[:, bi * 128:(bi + 1) * 128], bst, src[:, bi * 128:(bi + 1) * 128], bsrc,
                                     128, 128)
                    S.dma(lambda e, oi=oi, st=st: e.dma_start(
                        out=OPS[c0:c0 + 512, oi, :, :].rearrange("(b t) h k -> t b (h k)", t=128),
                        in_=st[:, 0:512].rearrange("p (b c) -> p b c", c=128)),
                        reads=[bst], writes=[b_OPS])

            def post(c0):
                ps, bps = bo_sum(YT[:, 0:512], bYT)
                S.op("dve", lambda e, ps=ps: e.scalar_tensor_tensor(out=TA[:, 0:512], in0=ps[:, :], scalar=-1.0 / 64,
                                                             in1=YT[:, 0:512], op0=ALU.mult, op1=ALU.add),
                     reads=[bps, bYT], writes=[bTA])
                S.op("dve", lambda e: e.tensor_tensor(out=TB[:, 0:512], in0=TA[:, 0:512], in1=TA[:, 0:512],
                                                      op=ALU.mult), reads=[bTA], writes=[bTB])
                ps, bps = bo_sum(TB[:, 0:512], bTB)
                S.op("act", lambda e, ps=ps: e.activation(out=TB[:, 0:512], in_=ps[:, :], func=AF.Sqrt,
                                                   bias=rwc[:, 24:25], scale=1.0 / 64),
                     reads=[bps, b_rwc], writes=[bTB])
                S.op("dve", lambda e: e.reciprocal(out=TB[:, 0:512], in_=TB[:, 0:512]), reads=[bTB], writes=[bTB])
                S.op("dve", lambda e: e.tensor_tensor(out=TA[:, 0:512], in0=TA[:, 0:512], in1=TB[:, 0:512],
                                                      op=ALU.mult), reads=[bTA, bTB], writes=[bTA])
                S.op("dve", lambda e: e.tensor_scalar(out=TA[:, 0:512], in0=TA[:, 0:512], scalar1=prm(5),
                                                      scalar2=prm(6), op0=ALU.mult, op1=ALU.add),
                     reads=[bTA, b_rwc], writes=[bTA])
                S.op("dve", lambda e: e.tensor_tensor(out=TA[:, 0:512], in0=TA[:, 0:512], in1=BON[:, 0:512],
                                                      op=ALU.add), reads=[bTA, bBON], writes=[bTA])
                S.op("dve", lambda e: e.tensor_tensor(out=o_bf[:, :], in0=TA[:, 0:512], in1=GT[:, 0:512],
                                                      op=ALU.mult), reads=[bTA, bGT], writes=[b_ob])
                S.dma(lambda e: e.dma_start(out=mix_loc[0:128, c0:c0 + 512], in_=o_bf[:, :]),
                      reads=[b_ob], writes=[b_mixloc])

            VT = XS[2]
            bVT = bXS[2]
            S.op("dve", lambda e: e.memset(Sst[:, :], 0.0), writes=[b_S])
            gi = 0
            for ch in range(SEQ // 512):
                c0 = ch * 512
                prep(c0, False)
                if not (RWDBG & 1):
                    S.op("dve", lambda e: e.memset(YT[:, 0:512], 0.0), writes=[bYT])
                for grp in (range(16) if RWDBG & 1 else []):
                    jx = gi % 2
                    gi += 1
                    t0 = c0 + grp * 32
                    S.dma(lambda e, jx=jx, t0=t0: e.dma_start(
                        out=X2all[jx].rearrange("p (t k) -> p t k", k=64),
                        in_=OPS[t0:t0 + 32, :, :, :].rearrange("t o h k -> (o h) t k")),
                        reads=[b_OPS], writes=[b_X2[jx]])
                    for half in range(4):
                        hb = 0
                        for o in range(5):
                            S.op("pe", lambda e, o=o, half=half, jx=jx: e.matmul(
                                Pb[o][:, :], lhsT=selt[:, o * 128:(o + 1) * 128],
                                rhs=X2all[jx][:, half * 512:(half + 1) * 512], start=True, stop=True),
                                reads=[b_sel, b_X2[jx]], writes=[b_bank[o][0]])
                        for st_ in range(8):
                            tl = grp * 32 + half * 8 + st_
                            cs_ = slice(st_ * 64, st_ * 64 + 64)
                            S.op("dve", lambda e, cs_=cs_: e.tensor_tensor(out=Tsc[:, :], in0=Sst[:, :],
                                                                           in1=Pb[0][:, cs_], op=ALU.mult),
                                 reads=[b_S, b_bank[0][hb]], writes=[b_Tsc])
                            S.op("dve", lambda e: e.tensor_reduce(out=sat[:, 0:1], in_=Tsc[:, :],
                                                                  axis=mybir.AxisListType.X, op=ALU.add),
                                 reads=[b_Tsc], writes=[b_sa])
                            S.op("dve", lambda e, cs_=cs_: e.tensor_tensor(out=Sst[:, :], in0=Sst[:, :],
                                                                           in1=Pb[1][:, cs_], op=ALU.mult),
                                 reads=[b_S, b_bank[1][hb]], writes=[b_S])
                            S.op("dve", lambda e, cs_=cs_: e.scalar_tensor_tensor(
                                out=Sst[:, :], in0=Pb[2][:, cs_], scalar=sat[:, 0:1], in1=Sst[:, :],
                                op0=ALU.mult, op1=ALU.add), reads=[b_S, b_sa, b_bank[2][hb]], writes=[b_S])
                            S.op("dve", lambda e, cs_=cs_, tl=tl: e.scalar_tensor_tensor(
                                out=Sst[:, :], in0=Pb[3][:, cs_], scalar=VT[:, tl:tl + 1], in1=Sst[:, :],
                                op0=ALU.mult, op1=ALU.add), reads=[b_S, bVT, b_bank[3][hb]], writes=[b_S])
                            S.op("dve", lambda e, cs_=cs_: e.tensor_tensor(out=Tsc[:, :], in0=Sst[:, :],
                                                                           in1=Pb[4][:, cs_], op=ALU.mult),
                                 reads=[b_S, b_bank[4][hb]], writes=[b_Tsc])
                            S.op("dve", lambda e, tl=tl: e.tensor_reduce(out=YT[:, tl:tl + 1], in_=Tsc[:, :],
                                                                         axis=mybir.AxisListType.X, op=ALU.add),
                                 reads=[b_Tsc], writes=[bYT])
                post(c0)
            S.dma(lambda e: e.dma_start(out=wkv_out[l, 0, :, :], in_=Sst[:, :]), reads=[b_S], writes=[b_wkv])
            c0 = SEQ
            prep(c0, True)
            S3v = S3[:, :].rearrange("p (b k) -> p b k", k=64)
            T3v = T3[:, :].rearrange("p (b k) -> p b k", k=64)
            if not (RWDBG & 2):
                S.op("dve", lambda e: e.memset(YT[:, 0:512], 0.0), writes=[bYT])
            for g8 in (range(NSEQ // 8) if RWDBG & 2 else []):
                b0 = g8 * 8
                jx = gi % 2
                gi += 1
                S.dma(lambda e, b0=b0: e.dma_start(out=S3v, in_=swkv[l, b0:b0 + 8, :, :].rearrange("b p k -> p b k")),
                      writes=[b_S3])
                for t in range(4):
                    S.dma(lambda e, jx=jx, b0=b0, t=t: e.dma_start(
                        out=X2all[jx][:, t * 512:(t + 1) * 512].rearrange("p (b k) -> p b k", k=64),
                        in_=OPS[c0 + 4 * b0 + t:c0 + 4 * b0 + 32:4, :, :, :].rearrange("b o h k -> (o h) b k")),
                        reads=[b_OPS], writes=[b_X2[jx]])
                for t in range(4):
                    for o in range(5):
                        S.op("pe", lambda e, o=o, t=t, jx=jx: e.matmul(
                            Pb[o][:, :], lhsT=selt[:, o * 128:(o + 1) * 128],
                            rhs=X2all[jx][:, t * 512:(t + 1) * 512], start=True, stop=True),
                            reads=[b_sel, b_X2[jx]], writes=[b_bank[o][0], b_bank[o][1]])
                    bk = lambda o: Pb[o][:, :].rearrange("p (b k) -> p b k", k=64)
                    bb = lambda o: [b_bank[o][0], b_bank[o][1]]
                    vcol = VT[:, b0 * 4 + t:b0 * 4 + t + 29:4]
                    ycol = YT[:, b0 * 4 + t:b0 * 4 + t + 29:4]
                    bc = lambda ap: ap.unsqueeze(2).to_broadcast([128, 8, 64])
                    S.op("dve", lambda e, bk=bk: e.tensor_tensor(out=T3v, in0=S3v, in1=bk(0), op=ALU.mult),
                         reads=[b_S3] + bb(0), writes=[b_T3])
                    S.op("dve", lambda e: e.tensor_reduce(out=sat[:, 0:8], in_=T3v, axis=mybir.AxisListType.X,
                                                          op=ALU.add), reads=[b_T3], writes=[b_sa])
                    S.op("dve", lambda e, bk=bk: e.tensor_tensor(out=S3v, in0=S3v, in1=bk(1), op=ALU.mult),
                         reads=[b_S3] + bb(1), writes=[b_S3])
                    S.op("dve", lambda e, bk=bk, bc=bc: e.tensor_tensor(out=T3v, in0=bk(2), in1=bc(sat[:, 0:8]),
                                                                        op=ALU.mult),
                         reads=[b_sa] + bb(2), writes=[b_T3])
                    S.op("dve", lambda e: e.tensor_tensor(out=S3v, in0=S3v, in1=T3v, op=ALU.add),
                         reads=[b_S3, b_T3], writes=[b_S3])
                    S.op("dve", lambda e, bk=bk, bc=bc, vcol=vcol: e.tensor_tensor(out=T3v, in0=bk(3), in1=bc(vcol),
                                                                                   op=ALU.mult),
                         reads=[bVT] + bb(3), writes=[b_T3])
                    S.op("dve", lambda e: e.tensor_tensor(out=S3v, in0=S3v, in1=T3v, op=ALU.add),
                         reads=[b_S3, b_T3], writes=[b_S3])
                    S.op("dve", lambda e, bk=bk: e.tensor_tensor(out=T3v, in0=S3v, in1=bk(4), op=ALU.mult),
                         reads=[b_S3] + bb(4), writes=[b_T3])
                    S.op("dve", lambda e, ycol=ycol: e.tensor_reduce(out=ycol, in_=T3v, axis=mybir.AxisListType.X,
                                                                     op=ALU.add), reads=[b_T3], writes=[bYT])
                S.dma(lambda e, b0=b0: e.dma_start(
                    out=wkv_out[l, 1 + b0:1 + b0 + 8, :, :].rearrange("b p k -> p b k"), in_=S3v),
                    reads=[b_S3], writes=[b_wkv])
            post(c0)

        for l in range(2):
            allgather_h()
            load_weight(w_in[l, :, :], KT, NIN, gidx=3 * l + 0)

            def sink_u(bi, n0, nr, ch, m, l=l):
                cs = slice(ch * 512, (ch + 1) * 512)
                S.dma(lambda e: e.dma_start(out=U[n0:n0 + nr, cs], in_=ost[m][0:nr, :]),
                      reads=[b_ost[m]], writes=[b_U])
                if bi == 5:
                    S.dma(lambda e: e.dma_start(out=kv_out[l, 0:64, cs], in_=ost[m][64:128, :]),
                          reads=[b_ost[m]], writes=[b_out])
                if bi == 6:
                    S.dma(lambda e: e.dma_start(out=kv_out[l, 64:128, cs], in_=ost[m][0:64, :]),
                          reads=[b_ost[m]], writes=[b_out])
                if bi < 5 and ch == ch_last:
                    S.dma(lambda e: e.dma_start(out=shift_out[l, n0:n0 + 128, 0:1], in_=ost[m][:, 511:512],
                                                allow_slow_non_contiguous=True),
                          reads=[b_ost[m]], writes=[b_out])
                if bi < 5 and ch == ch_s0:
                    k = rr["mm"] % 2
                    S.op("dve", lambda e: e.tensor_copy(out=tmpf[k][:, 0:NSEQ], in_=ost[m][:, 3:512:4]),
                         reads=[b_ost[m]], writes=[b_tmpf[k]])
                    S.dma(lambda e: e.dma_start(out=shift_out[l, n0:n0 + 128, 1:1 + NSEQ],
                                                in_=tmpf[k][:, 0:NSEQ]),
                          reads=[b_tmpf[k]], writes=[b_out])
                if bi == 7 and ch == ch_last:
                    S.dma(lambda e: e.dma_start(out=pooln_out[l, :, 0:15], in_=ost[m][:, 497:512]),
                          reads=[b_ost[m]], writes=[b_out])
                if bi == 7 and ch == ch_s0:
                    S.dma(lambda e: e.dma_start(out=pooln_out[l, :, 15:15 + NS], in_=ost[m][:, :]),
                          reads=[b_ost[m]], writes=[b_out])
            gemm(HALL, b_HALL, KT, 512, NIN, blocks128(NIN), sink_u, norm=True)
            S.dma(lambda e, l=l: e.dma_start(out=poolp_out[l, :, :, :], in_=spool[l, :, 4:15, :]),
                  writes=[b_out])

            S.barrier()
            mixer_rwkv(l)
            S.barrier()
            mixer_pool(l)
            S.barrier()
            mixer_sb(l)
            S.barrier()

            allgather(mix_loc, b_mixloc, MALL, b_MALL)
            load_weight(w_o[l, :, :], KT, 256)

            def sink_acc(bi, n0, nr, ch, m):
                cs = slice(ch * 512, (ch + 1) * 512)
                S.op("dve", lambda e: e.tensor_tensor(out=hT[:, bi, cs], in0=hT[:, bi, cs], in1=ost[m][:, :],
                                                      op=ALU.add),
                     reads=[b_ost[m], b_hT[bi]], writes=[b_hT[bi]])
            gemm(MALL, b_MALL, KT, 512, 256, blocks128(256), sink_acc, norm=False)

            allgather_h()
            load_weight(w_gu[l, :, :], KT, 2 * NFF, gidx=3 * l + 1)
            gu_blocks = []
            for f0 in range(0, NFF, 128):
                fr = min(128, NFF - f0)
                gu_blocks += [(2 * f0, fr), (2 * f0 + fr, fr)]

            def sink_gu(bi, n0, nr, ch, m):
                cs = slice(ch * 512, (ch + 1) * 512)
                fb = bi // 2
                k = fb % 2
                if bi % 2 == 0:
                    S.op("act", lambda e: e.activation(out=tmpf[k][0:nr, :], in_=ost[m][0:nr, :], func=AF.Silu),
                         reads=[b_ost[m]], writes=[b_tmpf[k]])
                else:
                    S.op("dve", lambda e: e.tensor_tensor(out=tmpb[k][0:nr, :], in0=tmpf[k][0:nr, :],
                                                          in1=ost[m][0:nr, :], op=ALU.mult),
                         reads=[b_ost[m], b_tmpf[k]], writes=[b_tmpb[k]])
                    S.dma(lambda e: e.dma_start(out=hid_loc[fb * 128:fb * 128 + nr, cs], in_=tmpb[k][0:nr, :]),
                          reads=[b_tmpb[k]], writes=[b_hidloc])
            gemm(HALL, b_HALL, KT, 512, 2 * NFF, gu_blocks, sink_gu, norm=True)
            allgather(hid_loc, b_hidloc, HIDALL, b_HIDALL)
            load_weight(w_dn[l, :, :], KF, 256)

            def sink_acc128(bi, n0, nr, ch, m):
                cs = slice(ch * 128, (ch + 1) * 128)
                S.op("dve", lambda e: e.tensor_tensor(out=hT[:, bi, cs], in0=hT[:, bi, cs], in1=ost[m][:, 0:128],
                                                      op=ALU.add),
                     reads=[b_ost[m], b_hT[bi]], writes=[b_hT[bi]])
            gemm(HIDALL, b_HIDALL, KF, 128, 256, blocks128(256), sink_acc128, norm=False)

            allgather_h()
            load_weight(w_pg[l, :, :], KT, 256, gidx=3 * l + 2)

            def sink_sg(bi, n0, nr, ch, m):
                cs = slice(ch * 512, (ch + 1) * 512)
                k = bi % 2
                S.op("act", lambda e: e.activation(out=tmpf[k][:, :], in_=ost[m][:, :], func=AF.Sigmoid),
                     reads=[b_ost[m]], writes=[b_tmpf[k]])
                S.dma(lambda e: e.dma_start(out=sg[n0:n0 + 128, cs], in_=tmpf[k][:, :]),
                      reads=[b_tmpf[k]], writes=[b_sg])
            gemm(HALL, b_HALL, KT, 512, 256, blocks128(256), sink_sg, norm=True)
            load_weight(w_pp[l, :, :], 2, 256)

            def sink_ple(bi, n0, nr, ch, m):
                cs = slice(ch * 512, (ch + 1) * 512)
                k = bi % 2
                S.dma(lambda e: e.dma_start(out=tmpf[k][:, :], in_=sg[n0:n0 + 128, cs]),
                      reads=[b_sg], writes=[b_tmpf[k]])
                S.op("dve", lambda e: e.tensor_tensor(out=tmpf[k][:, :], in0=tmpf[k][:, :], in1=ost[m][:, :],
                                                      op=ALU.mult),
                     reads=[b_ost[m], b_tmpf[k]], writes=[b_tmpf[k]])
                S.op("dve", lambda e: e.tensor_tensor(out=hT[:, bi, cs], in0=hT[:, bi, cs], in1=tmpf[k][:, :],
                                                      op=ALU.add),
                     reads=[b_tmpf[k], b_hT[bi]], writes=[b_hT[bi]])
            gemm(pb[l, :, :], b_pb, 2, 512, 256, blocks128(256), sink_ple, norm=False)

        allgather_h()
        for ch in range(T // 512):
            j = ch % 2
            cs = slice(ch * 512, (ch + 1) * 512)
            load_chunk(HALL, b_HALL, KT, 512, ch, j)
            chunk_rstd(j)
            for ft in range(2):
                m = rr["mm"] % 2
                rr["mm"] += 1
                S.op("dve", lambda e, ft=ft, m=m, cs=cs, j=j: e.scalar_tensor_tensor(
                    out=ost[m][:, :], in0=hT[:, ft, cs], scalar=gf[:, ft:ft + 1], in1=rstd[j][:, :],
                    op0=ALU.mult, op1=ALU.mult),
                    reads=[b_hT[ft], b_gf, b_rstd[j]], writes=[b_ost[m]])
                S.dma(lambda e, ft=ft, m=m, cs=cs: e.dma_start(out=yT_out[ft * 128:(ft + 1) * 128, cs],
                                                               in_=ost[m][:, :]),
                      reads=[b_ost[m]], writes=[b_out])
        S.finish()
    return nc


H_A, HD = 16, 64
D_A, D_B, D_C = 1024, 512, 512
RW = 3 * D_A + 256
I_SB = RW
I_PO = RW + 3 * D_B


def in_cols(c):
    r = list(range(128 * c, 128 * c + 128))
    k = [D_A + i for i in r]
    v = [2 * D_A + i for i in r]
    wa = list(range(3 * D_A, 3 * D_A + 128))
    g = list(range(3 * D_A + 128, 3 * D_A + 256))
    q = list(range(I_SB + 64 * c, I_SB + 64 * c + 64))
    kk = [D_B + i for i in q]
    vv = [2 * D_B + i for i in q] + [-1] * 64
    po = list(range(I_PO + 128 * (c // 2), I_PO + 128 * (c // 2) + 128))
    return np.array(r + k + v + wa + g + q + kk + vv + po)


def take_cols(w, cols):
    out = np.zeros(w.shape[:-1] + (len(cols),), w.dtype)
    m = cols >= 0
    out[..., m] = w[..., cols[m]]
    return out


def mix_perm():
    perm = []
    for c in range(NCORES):
        perm += list(range(128 * c, 128 * c + 128))
        perm += list(range(D_A + 64 * c, D_A + 64 * c + 64))
        o = D_A + D_B + 128 * (c // 2) + 64 * (c % 2)
        perm += list(range(o, o + 64))
    return np.array(perm)


def gu_cols(c):
    g, u = [], []
    out_g, out_u = [], []
    cols = []
    for f0 in range(0, NFF, 128):
        fr = min(128, NFF - f0)
        f = np.arange(NFF * c + f0, NFF * c + f0 + fr)
        cols.append(("g", f))
        cols.append(("u", f))
    return cols


def rw_blocks(c):
    r = np.arange(128 * c, 128 * c + 128)
    return [r, D_A + r, 2 * D_A + r, np.arange(3 * D_A, 3 * D_A + 128), np.arange(3 * D_A + 128, 3 * D_A + 256)]


def _sel10():
    m = np.zeros((10, 5, 2, 64), np.float32)
    for o in range(5):
        for h in range(2):
            m[2 * o + h, o, h, :] = 1.0
    return m.reshape(10, 640)


SEL10 = _sel10()


def kernel(**inp):
    f32 = np.float32
    x_prompt, x_sample = inp["x_prompt"], inp["x_sample"]
    SEQ = x_prompt.shape[1]
    NB = x_sample.shape[0]
    NS = NB * x_sample.shape[1]
    T = SEQ + NS
    assert T % 512 == 0 and NS == 512 and SEQ % 512 == 0
    xall = np.concatenate([x_prompt[0], x_sample.reshape(NS, D)], axis=0)
    xT_full = np.ascontiguousarray(xall.T)
    pT = np.ascontiguousarray(np.concatenate(
        [inp["p_prompt"][:, 0], inp["p_sample"].reshape(2, NS, 256)], axis=1).transpose(0, 2, 1))
    gl = lambda g: g.reshape(D // 128, 128).T
    gvec = np.ascontiguousarray(np.stack(
        [gl(inp[k][l]) for l in range(2) for k in ("g_mix", "g_ffn", "g_ple")]).astype(f32))
    perm = mix_perm()
    NPG = inp["page_table"].shape[1]
    NPOOL = inp["cache_k"].shape[1]
    nc = build_program(T, SEQ, NS, NPG, NPOOL)
    ptab = np.ascontiguousarray(inp["page_table"].reshape(1, -1).astype(np.int32))
    in_maps = []
    for c in range(NCORES):
        o = slice(256 * c, 256 * c + 256)
        w_gu = np.concatenate(
            [(inp["w_gate"] if k == "g" else inp["w_up"])[:, :, f] for k, f in gu_cols(c)], axis=2)
        in_maps.append({
            "xT": np.ascontiguousarray(xT_full[o]),
            "pT": pT,
            "w_in": np.ascontiguousarray(take_cols(inp["w_in"], in_cols(c))),
            "w_o": np.ascontiguousarray(inp["w_o"][:, perm][:, :, o]),
            "w_gu": np.ascontiguousarray(w_gu),
            "w_dn": np.ascontiguousarray(inp["w_down"][:, :, o]),
            "w_pg": np.ascontiguousarray(inp["w_pg"][:, :, o]),
            "w_pp": np.ascontiguousarray(inp["w_pp"][:, :, o]),
            "gvec": gvec,
            "gfin": np.ascontiguousarray(inp["g_final"][o].reshape(2, 128).T.astype(f32)),
            "spool": np.ascontiguousarray(inp["state_pool"][:, :, :, 128 * (c // 2):128 * (c // 2) + 128]),
            "spT": np.ascontiguousarray(
                inp["state_pool"][:, :, :, 128 * (c // 2):128 * (c // 2) + 128].transpose(0, 3, 1, 2)),
            "selw": np.ascontiguousarray(np.tile(np.eye(4, dtype=f32)[c // 2][None, :], (128, 1))),
            "wpool": np.ascontiguousarray(inp["w_pool"][:, c // 2][:, :, 64 * (c % 2):64 * (c % 2) + 64]),
            "psc": np.ascontiguousarray(
                inp["pool_scale"][:, 128 * (c // 2) + 64 * (c % 2):128 * (c // 2) + 64 * (c % 2) + 64].T),
            "sbb": np.ascontiguousarray(np.tile(inp["sb_bias"][:, c][None, :], (128, 1)).astype(f32)),
            "ck0": np.ascontiguousarray(inp["cache_k"][0, :, :, c, :]).reshape(NPOOL * 128, 64),
            "ck1": np.ascontiguousarray(inp["cache_k"][1, :, :, c, :]).reshape(NPOOL * 128, 64),
            "cv0": np.ascontiguousarray(inp["cache_v"][0, :, :, c, :]).reshape(NPOOL * 128, 64),
            "cv1": np.ascontiguousarray(inp["cache_v"][1, :, :, c, :]).reshape(NPOOL * 128, 64),
            "ptab": ptab,
            "rw_mu": np.ascontiguousarray(np.stack(
                [inp["mu_shift"][l, blk] for l in range(2) for blk in rw_blocks(c)], axis=1).astype(f32)),
            "rw_prm": np.ascontiguousarray(np.stack(
                [v[l].reshape(-1)[128 * c:128 * c + 128] for l in range(2)
                 for v in (inp["w0"], inp["a0"], inp["k_k"], inp["k_a"], inp["r_k"], inp["ln_w"], inp["ln_b"])],
                axis=1).astype(f32)),
            "rw_lw": np.ascontiguousarray(np.concatenate(
                [inp["w_lora_up"][:, :, 128 * c:128 * c + 128], inp["a_lora_up"][:, :, 128 * c:128 * c + 128]],
                axis=1)),
            "rw_gw": np.ascontiguousarray(inp["g_lora_up"][:, :, 128 * c:128 * c + 128]),
            "sshT": np.ascontiguousarray(np.concatenate(
                [inp["state_shift"][:, :, blk] for blk in rw_blocks(c)], axis=2).transpose(0, 2, 1)),
            "swkv": np.ascontiguousarray(inp["state_wkv"][:, :, 2 * c:2 * c + 2].reshape(2, NB, 128, 64)),
            "sel10": SEL10,
        })
    res = run_bass_kernel_spmd(nc, in_maps, core_ids=list(range(NCORES))).results

    Y = np.concatenate([r["yT_out"] for r in res], axis=0).T
    y_prompt = np.ascontiguousarray(Y[:SEQ][None])
    y_sample = np.ascontiguousarray(Y[SEQ:].reshape(NB, 4, D))
    kT = np.stack([r["kv_out"][:, 0:64, :] for r in res], axis=1)
    vT = np.stack([r["kv_out"][:, 64:128, :] for r in res], axis=1)
    k_all = kT.transpose(0, 3, 1, 2)
    v_all = vT.transpose(0, 3, 1, 2)
    nk_p = np.ascontiguousarray(k_all[:, None, :SEQ])
    nv_p = np.ascontiguousarray(v_all[:, None, :SEQ])
    nk_s = np.ascontiguousarray(k_all[:, SEQ:].reshape(2, NB, 4, 8, 64))
    nv_s = np.ascontiguousarray(v_all[:, SEQ:].reshape(2, NB, 4, 8, 64))
    sh = np.zeros((2, 1 + NB, RW), f32)
    for c, r in enumerate(res):
        so = r["shift_out"]
        for j in range(3):
            sh[:, :, j * D_A + 128 * c:j * D_A + 128 * c + 128] = so[:, 128 * j:128 * j + 128].transpose(0, 2, 1)
        if c == 0:
            sh[:, :, 3 * D_A:3 * D_A + 256] = so[:, 384:640].transpose(0, 2, 1)
    shift_p = np.ascontiguousarray(sh[:, 0:1])
    shift_s = np.ascontiguousarray(sh[:, 1:])
    pool_p = np.zeros((2, 1, 15, D_C), f32)
    pool_s = np.zeros((2, NB, 15, D_C), f32)
    for g in range(4):
        r = res[2 * g]
        cs = slice(128 * g, 128 * g + 128)
        pool_p[:, 0, :, cs] = r["pooln_out"][:, :, 0:15].transpose(0, 2, 1)
        pool_s[:, :, 0:11, cs] = r["poolp_out"]
        pool_s[:, :, 11:15, cs] = r["pooln_out"][:, :, 15:].reshape(2, 128, NB, 4).transpose(0, 2, 3, 1)
    wk = np.stack([r["wkv_out"].reshape(2, 1 + NB, 2, HD, HD) for r in res], axis=2)
    wk = wk.reshape(2, 1 + NB, H_A, HD, HD)
    wkv_p = np.ascontiguousarray(wk[:, 0:1])
    wkv_s = np.ascontiguousarray(wk[:, 1:])
    return (y_prompt, y_sample, nk_p, nv_p, nk_s, nv_s, wkv_p, wkv_s, shift_p, shift_s, pool_p, pool_s)
```
